# Optimizing a Trainium2 kernel written in Bass

```python
import jax, jax.numpy as jnp
from jax import lax
import numpy as np

D_MODEL = 2048
BATCH = 2
SEQ = 16384
DEPTH = 2

HG_HEADS = 8
HG_KDIM = 128
HG_VDIM = 128
HG_QF = HG_HEADS * HG_KDIM
HG_WIDTH = HG_HEADS * HG_VDIM
HG_CHUNK = 64
AT_HEADS = 16
AT_KV_HEADS = 2
AT_HEAD_DIM = 64
AT_GROUP = AT_HEADS // AT_KV_HEADS
AT_WIDTH = AT_HEADS * AT_HEAD_DIM
AT_KV_WIDTH = AT_KV_HEADS * AT_HEAD_DIM
WINDOW = 128
BLOCK = 128
D_FF = 4 * D_MODEL
EPS = 1e-6

SPLIT_SIZES = (HG_QF, HG_QF, HG_WIDTH, HG_WIDTH, AT_WIDTH, AT_KV_WIDTH, AT_KV_WIDTH, D_MODEL, D_MODEL)
D_IN = sum(SPLIT_SIZES)

kernel_name = "hybrid_hgrn2_swa_sink_gated"


def rmsnorm(x, g):
    xf = x.astype(jnp.float32)
    y = xf * lax.rsqrt(jnp.mean(xf * xf, axis=-1, keepdims=True) + EPS)
    return (y * g.astype(jnp.float32)).astype(x.dtype)


def alibi_slopes(n):
    return jnp.asarray(2.0 ** (-8.0 * (np.arange(n, dtype=np.float32) + 1.0) / n), dtype=jnp.float32)


def hgrn2(q, f_logit, i, lb):
    bsz, t_len, _ = q.shape
    n_chunks = t_len // HG_CHUNK
    c = HG_CHUNK
    lb = lb.astype(jnp.float32)
    qf = jax.nn.silu(q.astype(jnp.float32))
    fg = lb + (1.0 - lb) * jax.nn.sigmoid(f_logit.astype(jnp.float32))
    kf = 1.0 - fg
    logf = jnp.log(fg)

    def chunks(a, d):
        return a.reshape(bsz, n_chunks, c, HG_HEADS, d).transpose(0, 1, 3, 2, 4)

    q_c = chunks(qf, HG_KDIM)
    k_c = chunks(kf, HG_KDIM)
    g_c = chunks(logf, HG_KDIM)
    v_c = chunks(i.astype(jnp.float32), HG_VDIM)
    cum = jnp.cumsum(g_c, axis=3)
    ref = cum[:, :, :, c // 2 - 1:c // 2]
    a = jnp.einsum('bnhck,bnhsk->bnhcs', q_c * jnp.exp(cum - ref), k_c * jnp.exp(ref - cum))
    causal = jnp.tril(jnp.ones((c, c), dtype=bool))
    a = jnp.where(causal, a, 0.0)
    o_intra = jnp.einsum('bnhcs,bnhsv->bnhcv', a, v_c)
    last = cum[:, :, :, -1:]
    q_out = q_c * jnp.exp(cum)
    k_state = k_c * jnp.exp(last - cum)
    decay_last = jnp.exp(last[:, :, :, 0, :])

    def step(state, xs):
        qo, ks, vs, dl = xs
        o = jnp.einsum('bhck,bhkv->bhcv', qo, state)
        state = dl[..., None] * state + jnp.einsum('bhck,bhcv->bhkv', ks, vs)
        return state, o

    xs = (jnp.moveaxis(q_out, 1, 0), jnp.moveaxis(k_state, 1, 0),
          jnp.moveaxis(v_c, 1, 0), jnp.moveaxis(decay_last, 1, 0))
    s0 = jnp.zeros((bsz, HG_HEADS, HG_KDIM, HG_VDIM), jnp.float32)
    _, o_inter = lax.scan(step, s0, xs)
    o = o_intra + jnp.moveaxis(o_inter, 0, 1)
    return o.transpose(0, 1, 3, 2, 4).reshape(bsz, t_len, HG_HEADS, HG_VDIM)


def swa_sinks(q, k, v, sinks):
    bsz, t_len = q.shape[0], q.shape[1]
    nb = t_len // BLOCK
    qb = q.reshape(bsz, nb, BLOCK, AT_KV_HEADS, AT_GROUP, AT_HEAD_DIM)
    pad = jnp.zeros((bsz, BLOCK, AT_KV_HEADS, AT_HEAD_DIM), k.dtype)

    def band(a):
        ap = jnp.concatenate([pad, a], axis=1).reshape(bsz, nb + 1, BLOCK, AT_KV_HEADS, AT_HEAD_DIM)
        return jnp.concatenate([ap[:, :-1], ap[:, 1:]], axis=2)

    kb = band(k)
    vb = band(v)
    scale = AT_HEAD_DIM ** -0.5
    s = jnp.einsum('bnqhgd,bnkhd->bnhgqk', qb, kb).astype(jnp.float32) * scale
    qi = jnp.arange(BLOCK)[:, None]
    kj = jnp.arange(2 * BLOCK)[None, :]
    dist = qi + BLOCK - kj
    in_band = (dist >= 0) & (dist < WINDOW)
    blk = jnp.arange(nb)[:, None, None]
    valid = in_band[None] & ((blk > 0) | (kj[None] >= BLOCK))
    slopes = alibi_slopes(AT_HEADS)
    bias = -slopes[:, None, None] * dist.astype(jnp.float32)[None]
    s = s + bias.reshape(AT_KV_HEADS, AT_GROUP, BLOCK, 2 * BLOCK)[None, None]
    s = jnp.where(valid[None, :, None, None], s, -jnp.inf)
    sink = sinks.astype(jnp.float32).reshape(AT_KV_HEADS, AT_GROUP)[None, None, :, :, None, None]
    m = jnp.maximum(jnp.max(s, axis=-1, keepdims=True), sink)
    p = jnp.exp(s - m)
    p = p / (jnp.sum(p, axis=-1, keepdims=True) + jnp.exp(sink - m))
    o = jnp.einsum('bnhgqk,bnkhd->bnqhgd', p.astype(v.dtype), vb)
    return o.reshape(bsz, t_len, AT_WIDTH)


def setup_inputs(seed: int = 0) -> dict:
    key = jax.random.key(seed)
    ks = jax.random.split(key, 14)
    f32 = jnp.float32
    L = DEPTH
    return {
        "x": jax.random.normal(ks[0], (BATCH, SEQ, D_MODEL), f32),
        "norm_mix": 1.0 + 0.1 * jax.random.normal(ks[1], (L, D_MODEL), f32),
        "w_in": jax.random.normal(ks[2], (L, D_MODEL, D_IN), f32) * D_MODEL ** -0.5,
        "lb_logits": 0.5 * jax.random.normal(ks[3], (L, HG_QF), f32),
        "hg_norm": 1.0 + 0.1 * jax.random.normal(ks[4], (L, HG_VDIM), f32),
        "q_norm": 1.0 + 0.1 * jax.random.normal(ks[5], (L, AT_HEAD_DIM), f32),
        "k_norm": 1.0 + 0.1 * jax.random.normal(ks[6], (L, AT_HEAD_DIM), f32),
        "sinks": 0.5 * jax.random.normal(ks[7], (L, AT_HEADS), f32),
        "w_hg_out": jax.random.normal(ks[8], (L, HG_WIDTH, D_MODEL), f32) * HG_WIDTH ** -0.5,
        "w_at_out": jax.random.normal(ks[9], (L, AT_WIDTH, D_MODEL), f32) * AT_WIDTH ** -0.5,
        "w_out": jax.random.normal(ks[10], (L, D_MODEL, D_MODEL), f32) * D_MODEL ** -0.5,
        "norm_ffn": 1.0 + 0.1 * jax.random.normal(ks[11], (L, D_MODEL), f32),
        "w_up": jax.random.normal(ks[12], (L, D_MODEL, D_FF), f32) * D_MODEL ** -0.5,
        "w_down": jax.random.normal(ks[13], (L, D_FF, D_MODEL), f32) * (0.5 * D_FF ** -0.5),
    }


def reference(x, norm_mix, w_in, lb_logits, hg_norm, q_norm, k_norm, sinks,
              w_hg_out, w_at_out, w_out, norm_ffn, w_up, w_down):
    bsz, t_len, _ = x.shape
    lbp = jax.nn.softmax(lb_logits.astype(jnp.float32), axis=0)
    lower_bounds = jnp.cumsum(lbp, axis=0) - lbp[0:1]
    offsets = [int(v) for v in np.cumsum(SPLIT_SIZES)[:-1]]
    for l in range(DEPTH):
        h = rmsnorm(x, norm_mix[l])
        z = h @ w_in[l]
        hq, hf, hi, hgate, aq, ak, av, g_hg, g_at = jnp.split(z, offsets, axis=-1)
        o_hg = hgrn2(hq, hf, hi, lower_bounds[l]).astype(x.dtype)
        o_hg = rmsnorm(o_hg, hg_norm[l]) * jax.nn.silu(hgate.reshape(bsz, t_len, HG_HEADS, HG_VDIM))
        y_hg = o_hg.reshape(bsz, t_len, HG_WIDTH) @ w_hg_out[l]
        qa = rmsnorm(aq.reshape(bsz, t_len, AT_HEADS, AT_HEAD_DIM), q_norm[l])
        ka = rmsnorm(ak.reshape(bsz, t_len, AT_KV_HEADS, AT_HEAD_DIM), k_norm[l])
        va = av.reshape(bsz, t_len, AT_KV_HEADS, AT_HEAD_DIM)
        y_at = swa_sinks(qa, ka, va, sinks[l]) @ w_at_out[l]
        mixed = jax.nn.sigmoid(g_hg) * y_hg + jax.nn.sigmoid(g_at) * y_at
        x = x + mixed @ w_out[l]
        h2 = rmsnorm(x, norm_ffn[l])
        x = x + jnp.square(jax.nn.relu(h2 @ w_up[l])) @ w_down[l]
    return x
```

```python
import contextlib
import numpy as np
import concourse.bass as bass
import concourse.mybir as mybir
from concourse.bass_utils import run_bass_kernel_spmd

F32 = mybir.dt.float32
BF16 = mybir.dt.bfloat16
AF = mybir.ActivationFunctionType
ALU = mybir.AluOpType

D = 2048
DIN = 9472
DFF = 8192
L = 2
TB = 512
EPS = 1e-6
O_Q, O_F, O_I, O_G, O_AQ, O_AK, O_AV, O_GH, O_GA = 0, 1024, 2048, 3072, 4096, 5120, 5248, 5376, 7424
NEG = -30000.0

ENGS = ("tensor", "vector", "scalar", "gpsimd", "sync")


class Buf:
    __slots__ = ("name", "w", "r")

    def __init__(self, name):
        self.name = name
        self.w = None
        self.r = {}


class Tl:
    __slots__ = ("ap", "bufs")

    def __init__(self, ap, bufs):
        self.ap = ap
        self.bufs = bufs

    def __getitem__(self, k):
        return self.ap[k]


class Op:
    __slots__ = ("eng", "fn", "waits", "idx", "dma", "dsem", "dval")


class Prog:
    def __init__(self, nc, n_dma_sems=8):
        self.nc = nc
        self.ops = {e: [] for e in ENGS}
        self.cnt = {e: 0 for e in ENGS}
        self.needed = {e: set() for e in ENGS}
        self.waited = {e: {} for e in ENGS}
        self.n_dma_sems = n_dma_sems
        self.dma_rr = {e: 0 for e in ENGS}
        self.dma_val = {}
        self.final_dma = []

    def _add_wait(self, op, ev):
        if ev is None:
            return
        kind, key, val = ev
        if kind == "e" and key == op.eng and key == "tensor":
            return
        w = self.waited[op.eng]
        k = (kind, key)
        if w.get(k, 0) >= val:
            return
        w[k] = val
        op.waits.append(ev)
        if kind == "e":
            self.needed[key].add(val)

    def op(self, eng, fn, reads=(), writes=(), dma=False, final=False, inc=16):
        o = Op()
        o.eng = eng
        o.fn = fn
        o.waits = []
        o.dma = dma
        rb = [b for t in reads for b in t.bufs]
        wb = [b for t in writes for b in t.bufs]
        for b in rb:
            self._add_wait(o, b.w)
        for b in wb:
            self._add_wait(o, b.w)
            for (kk, ky), vv in b.r.items():
                self._add_wait(o, (kk, ky, vv))
        if dma:
            k = self.dma_rr[eng]
            self.dma_rr[eng] = (k + 1) % self.n_dma_sems
            key = (eng, k)
            prev = self.dma_val.get(key, 0)
            if prev:
                self._add_wait(o, ("d", key, prev))
            self.dma_val[key] = prev + inc
            o.dsem = key
            o.dval = inc
            ev = ("d", key, prev + inc)
            o.idx = None
            if final:
                self.final_dma.append(ev)
        else:
            self.cnt[eng] += 1
            o.idx = self.cnt[eng]
            ev = ("e", eng, o.idx)
        for b in rb:
            b.r[(ev[0], ev[1])] = ev[2]
        for b in wb:
            b.w = ev
            b.r = {}
        self.ops[eng].append(o)
        return ev

    def emit(self):
        nc = self.nc
        EP = 30000
        remap = {}
        nep = {}
        for e in ENGS:
            remap[e] = {i: (n // EP, n % EP + 1) for n, i in enumerate(sorted(self.needed[e]))}
            nep[e] = max(1, (len(self.needed[e]) + EP - 1) // EP)
        fin = Op()
        fin.eng = "sync"; fin.fn = None; fin.waits = []; fin.dma = False; fin.idx = None
        for ev in self.final_dma:
            self._add_wait(fin, ev)
        self.ops["sync"].append(fin)
        with contextlib.ExitStack() as st:
            esem = {e: [st.enter_context(nc.semaphore("es_%s%d" % (e, k))) for k in range(nep[e])] for e in ENGS}
            dsem = {}
            for key in self.dma_val:
                dsem[key] = st.enter_context(nc.semaphore("ds_%s%d" % key))
            block = st.enter_context(nc.Block())

            def run(eng_name, eng):
                for o in self.ops[eng_name]:
                    for kind, key, val in o.waits:
                        if kind == "e":
                            ep, v = remap[key][val]
                            eng.wait_ge(esem[key][ep], v)
                        else:
                            eng.wait_ge(dsem[key], val)
                    if o.fn is None:
                        continue
                    ins = o.fn(eng)
                    if o.dma:
                        ins.then_inc(dsem[o.dsem], o.dval)
                    elif o.idx in remap[eng_name]:
                        ins.then_inc(esem[eng_name][remap[eng_name][o.idx][0]], 1)

            @block.tensor
            def _(eng):
                run("tensor", eng)

            @block.vector
            def _(eng):
                run("vector", eng)

            @block.scalar
            def _(eng):
                run("scalar", eng)

            @block.gpsimd
            def _(eng):
                run("gpsimd", eng)

            @block.sync
            def _(eng):
                run("sync", eng)


def host_consts():
    f = np.float32
    ident = np.eye(128, dtype=f)
    ones = np.ones((128, 128), f)
    bdiag = np.kron(np.eye(2, dtype=f), np.ones((64, 64), f))
    s = np.arange(128)[:, None]
    c = np.arange(128)[None, :]
    amask = ((s // 64 == c // 64) & (s <= c)).astype(f)
    mats = np.stack([ident, ones, bdiag, amask], 0)
    scanmask = np.ones((128, 512), f)
    scanmask[:, ::64] = 0
    slopes = (2.0 ** (-8.0 * (np.arange(16, dtype=np.float32) + 1.0) / 16)).astype(f)
    k = np.arange(128)[:, None, None]
    q = np.arange(128)[None, None, :]
    bias = np.zeros((2, 2, 128, 2, 4, 128), f)
    for g in range(2):
        for r in range(2):
            for j in range(4):
                h = g * 8 + 2 * j + r
                d0 = (q + 128 - k).astype(f)[:, 0, :]
                v0 = (k > q)[:, 0, :]
                d1 = (q - k).astype(f)[:, 0, :]
                v1 = (q >= k)[:, 0, :]
                bias[g, r, :, 0, j, :] = np.where(v0, -slopes[h] * d0, NEG)
                bias[g, r, :, 1, j, :] = np.where(v1, -slopes[h] * d1, NEG)
    return mats, scanmask, bias.reshape(4, 128, 1024)


class _Stop(Exception):
    pass


STOP = [99]


class Builder:
    def __init__(self, nblk):
        self.nblk = nblk
        self.stop = STOP[0]
        self.T = nblk * TB
        self.nc = bass.Bass("TRN2", target_bir_lowering=False)
        self.p = Prog(self.nc)
        self.st = contextlib.ExitStack()
        self.off = 0

    def sb(self, name, shape, dt, nbuf=1):
        esz = 4 if dt == F32 else 2
        n = int(np.prod(shape[1:]))
        sz = n * esz
        a = self.ARENA[:, self.off // 2:(self.off + sz) // 2]
        self.off += (sz + 63) // 64 * 64
        assert self.off <= self.arena_bytes, (name, self.off)
        if dt == F32:
            a = a.bitcast(F32)
        if len(shape) == 3:
            a = a.rearrange("p (a b) -> p a b", b=shape[2])
        return Tl(a, [Buf(name + str(i)) for i in range(nbuf)])

    def view(self, bufs, byte_off, shape, dt):
        esz = 4 if dt == F32 else 2
        n = int(np.prod(shape[1:]))
        a = self.ARENA[:, byte_off // 2:(byte_off + n * esz) // 2]
        if dt == F32:
            a = a.bitcast(F32)
        if len(shape) == 3:
            a = a.rearrange("p (a b) -> p a b", b=shape[2])
        return Tl(a, bufs)

    def ps_alloc(self):
        assert self.ps_free, "out of PSUM banks"
        return self.ps_free.pop(0)

    def ps_release(self, t):
        self.ps_free.append(t)

    def psb_alloc(self):
        assert self.psb_free
        return self.psb_free.pop(0)

    def psb_release(self, t):
        self.psb_free.append(t)

    def mm(self, out_t, out_ap, lhsT, rhs, start, stop, reads):
        self.p.op("tensor", lambda e: e.matmul(out_ap, lhsT, rhs, start=start, stop=stop),
                  reads=reads, writes=[out_t])

    def tr(self, out_t, out_ap, in_ap, ident_ap, reads):
        self.p.op("tensor", lambda e: e.transpose(out_ap, in_ap, ident_ap), reads=reads, writes=[out_t])

    def act(self, out_ap, in_ap, func, reads, writes, scale=None, bias=None, accum=None, eng="scalar"):
        kw = {}
        if scale is not None:
            kw["scale"] = scale
        if bias is not None:
            kw["bias"] = bias
        if accum is not None:
            kw["accum_out"] = accum
        self.p.op("scalar", lambda e: e.activation(out=out_ap, in_=in_ap, func=func, **kw),
                  reads=reads, writes=writes)

    def tt(self, eng, out_ap, in0, in1, op, reads, writes):
        self.p.op(eng, lambda e: e.tensor_tensor(out_ap, in0, in1, op), reads=reads, writes=writes)

    def ts(self, eng, out_ap, in0, s1, s2, op0, op1, reads, writes):
        self.p.op(eng, lambda e: e.tensor_scalar(out_ap, in0, s1, s2, op0, op1), reads=reads, writes=writes)

    def tsm(self, eng, out_ap, in0, s1, reads, writes):
        self.p.op(eng, lambda e: e.tensor_scalar_mul(out_ap, in0, s1), reads=reads, writes=writes)

    def stt(self, eng, out_ap, in0, sc, in1, op0, op1, reads, writes):
        self.p.op(eng, lambda e: e.scalar_tensor_tensor(out_ap, in0, sc, in1, op0, op1), reads=reads, writes=writes)

    def cp(self, eng, out_ap, in_ap, reads, writes):
        if eng == "scalar":
            self.p.op(eng, lambda e: e.activation(out=out_ap, in_=in_ap, func=AF.Copy), reads=reads, writes=writes)
        else:
            self.p.op(eng, lambda e: e.tensor_copy(out_ap, in_ap), reads=reads, writes=writes)

    def dma(self, eng, out_ap, in_ap, reads, writes, final=False):
        self.p.op(eng, lambda e: e.dma_start(out=out_ap, in_=in_ap), reads=reads, writes=writes, dma=True, final=final)

    def wpanel(self):
        t = self.WP[self.wp_i]
        self.wp_i = (self.wp_i + 1) % len(self.WP)
        return t

    def build(self):
        nc, st, T = self.nc, self.st, self.T
        dr = lambda n, s, k: nc.dram_tensor(n, s, F32, kind=k).ap()
        x = dr("x", [T, D], "ExternalInput")
        self.wgather = []

        def wshard(name, rows, cols):
            ext = dr(name, [rows // 8, cols], "ExternalInput")
            bnc = nc.dram_tensor(name + "_b", [rows // 8, cols], BF16).ap()
            full = nc.dram_tensor(name + "_f", [rows, cols], BF16).ap()
            bt = Tl(None, [Buf(name + "_b")]); ft = Tl(None, [Buf(name + "_f")])
            self.wgather.append((ext, bnc, full, bt, ft))
            per = rows // L
            return [full[l * per:(l + 1) * per, :] for l in range(L)], ft
        w_in, WIN_T = wshard("w_in", L * D, DIN)
        w_hg, WHG_T = wshard("w_hg_out", L * 1024, D)
        w_at, WAT_T = wshard("w_at_out", L * 1024, D)
        w_out, WOUT_T = wshard("w_out", L * D, D)
        w_up, WUP_T = wshard("w_up", L * D, DFF)
        w_dn, WDN_T = wshard("w_down", L * DFF, D)
        gmix = dr("gmix", [L, 128, 16], "ExternalInput")
        gffn = dr("gffn", [L, 128, 16], "ExternalInput")
        lbl = dr("lbl", [L, 128, 8], "ExternalInput")
        small = dr("small", [L, 128, 16], "ExternalInput")
        cmats = dr("cmats", [4, 128, 128], "ExternalInput")
        cscan = dr("cscan", [128, 512], "ExternalInput")
        cbias = dr("cbias", [4, 128, 1024], "ExternalInput")
        halo = dr("halo", [128, 8], "ExternalInput")
        cmask = dr("cmask", [128, 16], "ExternalInput")
        y = dr("y", [T, D], "ExternalOutput")
        PW = 1544
        cin = nc.dram_tensor("cin", [128, PW], F32).ap()
        cout = nc.dram_tensor("cout", [8 * 128, PW], F32).ap()
        cin_t = Tl(cin, [Buf("cin")]); cout_t = Tl(cout, [Buf("cout")])
        xs = nc.dram_tensor("xs", [T, D], F32).ap()
        self.xs_t = Tl(xs, [Buf("xs%d" % i) for i in range(self.nblk)])

        self.arena_bytes = 206 * 1024
        self.ARENA = st.enter_context(nc.sbuf_tensor("arena", [128, self.arena_bytes // 2], BF16))
        sb = self.sb
        self.X = sb("X", [128, 4, D], F32, 4)
        self.HT = sb("HT", [128, 16, TB], BF16, 16)
        ht_off = self.off - 16 * 1024
        self.WP = [sb("WP%d" % i, [128, 16, 512], BF16) for i in range(3)]
        self.wp_i = 0
        scr_off = self.off
        self.SCRB = [Buf("scr%d" % i) for i in range(64)]
        self.off += 64 * 1024
        self.scr_off = scr_off
        IDb = sb("IDb", [128, 128], BF16); ONESb = sb("ONESb", [128, 128], BF16)
        BDb = sb("BDb", [128, 128], BF16); AMb = sb("AMb", [128, 128], BF16)
        IDf = sb("IDf", [128, 128], F32)
        SCANM = sb("SCANM", [128, 512], F32)
        BIAS = [sb("BIAS%d" % i, [128, 1024], F32) for i in range(4)]
        HALO = sb("HALO", [128, 8], F32)
        EPST = sb("EPST", [128, 1], F32)
        ONE1 = sb("ONE1", [128, 1], F32)
        GMIX = sb("GMIX", [128, 2, 16], F32); GFFN = sb("GFFN", [128, 2, 16], F32)
        LBL = sb("LBL", [128, 2, 8], F32); OML = sb("OML", [128, 2, 8], F32)
        SMALL = sb("SMALL", [128, 2, 16], F32)
        QN2S = sb("QN2S", [128, 2], F32); ESINK = sb("ESINK", [128, 2, 8], F32)
        S = sb("S", [128, 8, 128], F32, 8); Sb = sb("Sb", [128, 8, 128], BF16, 8)
        KT = [sb("KT%d" % g, [128, 5, 128], BF16, 5) for g in range(2)]
        V2 = sb("V2", [128, 5, 256], BF16, 5)
        SSQ = sb("SSQ", [128, 8], F32); RSTD = sb("RSTD", [128, 8], F32)
        T1 = sb("T1", [128, 512], F32); T2 = sb("T2", [128, 512], F32)
        RS = sb("RS", [128, 512], F32); SQ = sb("SQ", [128, 512], BF16)
        CH = sb("CH", [128, 3, 8], F32)
        DEN = sb("DEN", [128, 512], F32)
        CM = sb("CM", [128, 16], F32); DL = sb("DL", [128, 8], F32)
        DLR = sb("DLR", [128, 8], F32); DM = sb("DM", [128, 8], F32)
        print("sbuf used KB", self.off / 1024)

        def scr(kb0, shape, dt, nb=None):
            esz = 4 if dt == F32 else 2
            nbytes = int(np.prod(shape[1:])) * esz
            k1 = kb0 + (nbytes + 1023) // 1024
            return self.view(self.SCRB[kb0:k1], scr_off + kb0 * 1024, shape, dt)

        def scr_split(kb0, shape, dt, n):
            esz = 4 if dt == F32 else 2
            per = int(np.prod(shape[2:])) * esz
            assert per % 1024 == 0
            kb = per // 1024
            full = self.view(self.SCRB[kb0:kb0 + n * kb], scr_off + kb0 * 1024, shape, dt)
            return full, [Tl(full.ap[:, i], self.SCRB[kb0 + i * kb: kb0 + (i + 1) * kb]) for i in range(n)]

        XN, XNt = scr_split(0, [128, 4, D], BF16, 4)
        VT, VTt = scr_split(0, [128, 4, 1024], BF16, 4)
        OHG, OHGh = scr_split(8, [128, 8, TB], BF16, 8)
        QH, QHj = scr_split(16, [128, 8, TB], BF16, 8)
        ATT, ATTj = scr_split(24, [128, 8, TB], BF16, 8)
        PT = [scr(32 + 4 * i, [128, 2, 512], BF16) for i in range(2)]
        QS = scr(44, [128, 512], F32); KK = scr(46, [128, 512], F32)
        LF = scr(48, [128, 512], F32); CUM = scr(50, [128, 512], F32)
        EQ = scr(52, [128, 512], F32); EK = scr(54, [128, 512], F32)
        QAb = scr(56, [128, 512], BF16); QOb = scr(57, [128, 512], BF16)
        KAb = scr(58, [128, 512], BF16); KSb = scr(59, [128, 512], BF16)
        KST = scr(60, [128, 4, 128], BF16); AM = scr(61, [128, 4, 128], BF16)
        SG1, SG1c = scr_split(40, [128, 4, TB], BF16, 4)
        SG2, SG2c = scr_split(62, [128, 2, TB], BF16, 2)
        MIX, MIXc = scr_split(44, [128, 16, TB], BF16, 16)
        UR = scr(44, [128, 1024], F32)
        KTL = scr(48, [128, 8, 256], F32); VTL = scr(56, [128, 8, 256], F32)
        SGA, SGAc = scr_split(32, [128, 4, TB], BF16, 4)
        AF_, AFc = scr_split(0, [128, 64, TB], BF16, 64)
        YT = [self.view(self.HT.bufs[2 * i:2 * i + 2], ht_off + 2048 * i, [128, 512], F32) for i in range(8)]

        self.ps_free = []
        for i in range(6):
            h = st.enter_context(nc.psum_tensor("ps%d" % i, [128, 512], F32))
            self.ps_free.append(Tl(h[:], [Buf("ps%d" % i)]))
        self.psb_free = []
        for i in range(2):
            hb = st.enter_context(nc.psum_tensor("psb%d" % i, [128, 1024], BF16))
            self.psb_free.append(Tl(hb[:, 0:512], [Buf("psb%d" % i)]))

        dma, act, tt, ts, tsm, stt, cp, mm, tr = self.dma, self.act, self.tt, self.ts, self.tsm, self.stt, self.cp, self.mm, self.tr
        p = self.p
        for ext, bnc, full, bt, ft in self.wgather:
            nr = ext.shape[0]
            for r0_ in range(0, nr, 128):
                dma("gpsimd", bnc[r0_:r0_ + 128, :], ext[r0_:r0_ + 128, :], [], [bt])
            p.op("gpsimd", lambda e, bnc=bnc, full=full: e.collective_compute(
                "AllGather", ALU.bypass, replica_groups=[list(range(8))], ins=[bnc.opt()], outs=[full.opt()]),
                reads=[bt], writes=[ft], dma=True, inc=1)
        dma("gpsimd", IDb.ap, cmats[0], [], [IDb]); dma("gpsimd", ONESb.ap, cmats[1], [], [ONESb])
        dma("gpsimd", BDb.ap, cmats[2], [], [BDb]); dma("gpsimd", AMb.ap, cmats[3], [], [AMb])
        dma("sync", IDf.ap, cmats[0], [], [IDf]); dma("sync", SCANM.ap, cscan, [], [SCANM])
        for i in range(4):
            dma("sync", BIAS[i].ap, cbias[i], [], [BIAS[i]])
        dma("sync", HALO.ap, halo, [], [HALO])
        dma("sync", CM.ap, cmask, [], [CM])
        for l in range(L):
            dma("sync", GMIX[:, l], gmix[l], [], [GMIX]); dma("sync", GFFN[:, l], gffn[l], [], [GFFN])
            dma("sync", LBL[:, l], lbl[l], [], [LBL]); dma("sync", SMALL[:, l], small[l], [], [SMALL])
        p.op("vector", lambda e: e.memset(EPST.ap, EPS), writes=[EPST])
        p.op("vector", lambda e: e.memset(ONE1.ap, 1.0), writes=[ONE1])
        p.op("vector", lambda e: e.memset(OML[:, 0], 1.0), writes=[OML])
        tt("vector", OML[:, 1], LBL[:, 0], LBL[:, 1], ALU.subtract, [LBL], [OML])
        act(OML[:, 1], OML[:, 1], AF.Sigmoid, [OML], [OML])
        for l in range(L):
            tsm("vector", QN2S[:, l:l + 1], SMALL[:, l, 1:2], 0.125, [SMALL], [QN2S])
            act(ESINK[:, l], SMALL[:, l, 8:16], AF.Exp, [SMALL], [ESINK])

        ev_alt = [0]

        def evac_eng():
            ev_alt[0] ^= 1
            return "vector" if ev_alt[0] else "scalar"

        def norm(g_t, l):
            for t in range(4):
                act(XNt[t].ap, self.X[:, t], AF.Square, [Tl(None, [self.X.bufs[t]])], [XNt[t], SSQ], accum=SSQ[:, t:t + 1])
            act(RSTD[:, 0:4], SSQ[:, 0:4], AF.Ln, [SSQ, EPST], [RSTD], scale=1.0 / D, bias=EPST[:, 0:1])
            act(RSTD[:, 0:4], RSTD[:, 0:4], AF.Exp, [RSTD], [RSTD], scale=-0.5)
            for t in range(4):
                eng = "vector" if t % 2 == 0 else "gpsimd"
                tsm(eng, XNt[t].ap, self.X[:, t], RSTD[:, t:t + 1], [Tl(None, [self.X.bufs[t]]), RSTD], [XNt[t]])
            for kc in range(16):
                pb = self.psb_alloc()
                for t in range(4):
                    tr(pb, pb[:, t * 128:(t + 1) * 128], XN[:, t, kc * 128:(kc + 1) * 128], IDb.ap, [XNt[t], IDb])
                hk = Tl(None, [self.HT.bufs[kc]])
                if kc % 2 == 0:
                    tsm("vector", self.HT[:, kc], pb.ap, g_t[:, l, kc:kc + 1], [pb, g_t], [hk])
                else:
                    act(self.HT[:, kc], pb.ap, AF.Copy, [pb, g_t], [hk], scale=g_t[:, l, kc:kc + 1])
                self.psb_release(pb)

        def load_cols(wt, dst_col, src2d, c0, ncols, dep=None):
            src = src2d.rearrange("(kc p) n -> p kc n", p=128)[:, :, c0:c0 + ncols]
            dma("gpsimd", wt[:, :, dst_col:dst_col + ncols], src, [WIN_T if dep is None else dep], [wt])

        HTall = Tl(None, self.HT.bufs)

        def gemm_fm(wt, col0, evac, nk=16, rhs_fn=None, rhs_reads=None):
            ps = self.ps_alloc()
            for kc in range(nk):
                rhs = self.HT[:, kc] if rhs_fn is None else rhs_fn(kc)
                mm(ps, ps.ap, wt[:, kc, col0:col0 + 128], rhs, kc == 0, kc == nk - 1,
                   [wt] + ([HTall] if rhs_reads is None else rhs_reads))
            evac(ps)
            self.ps_release(ps)

        def rms_fm(ps, ones_t, inv_n):
            act(SQ.ap, ps.ap, AF.Square, [ps], [SQ])
            p2 = self.ps_alloc()
            mm(p2, p2.ap, ones_t.ap, SQ.ap, True, True, [ones_t, SQ])
            act(RS.ap, p2.ap, AF.Ln, [p2, EPST], [RS], scale=inv_n, bias=EPST[:, 0:1])
            self.ps_release(p2)
            act(RS.ap, RS.ap, AF.Exp, [RS], [RS], scale=-0.5)

        for l in range(L):
            src = x if l == 0 else xs
            dst = xs if l == 0 else y
            wl = w_in[l]
            S2d = S.ap.rearrange("p h v -> p (h v)")
            p.op("vector", lambda e: e.memset(S.ap, 0.0), writes=[S])
            p.op("vector", lambda e: e.memset(DL.ap, 0.0), writes=[DL])
            for blk in range(self.nblk):
                r0 = blk * TB
                xsb = Tl(None, [self.xs_t.bufs[blk]])
                for t in range(4):
                    dma("sync", self.X[:, t], src[r0 + t * 128:r0 + (t + 1) * 128, :],
                        [xsb] if l == 1 else [], [Tl(None, [self.X.bufs[t]])])
                norm(GMIX, l)
                for hp in range(2):
                    wt = self.wpanel()
                    load_cols(wt, 0, wl, O_I + hp * 512, 512)
                    for t in range(4):
                        ps = self.ps_alloc()
                        for kc in range(16):
                            mm(ps, ps.ap, self.HT[:, kc, t * 128:(t + 1) * 128], wt[:, kc, :], kc == 0, kc == 15, [wt, HTall])
                        cp(evac_eng(), VT[:, t, hp * 512:(hp + 1) * 512], ps.ap, [ps], [VTt[t]])
                        self.ps_release(ps)
                for hp in range(2):
                    wt = self.wpanel()
                    load_cols(wt, 0, wl, O_F + hp * 512, 512)
                    for hh in range(4):
                        h = hp * 4 + hh
                        gemm_fm(wt, hh * 128, lambda ps: act(KK.ap, ps.ap, AF.Sigmoid, [ps], [KK], scale=-1.0))
                        tsm("vector", KK.ap, KK.ap, OML[:, l, h:h + 1], [KK, OML], [KK])
                        act(LF.ap, KK.ap, AF.Ln, [KK, ONE1], [LF], scale=-1.0, bias=ONE1[:, 0:1])
                        p.op("vector", lambda e: e.tensor_tensor_scan(CUM.ap, SCANM.ap, LF.ap, 0.0, ALU.mult, ALU.add),
                             reads=[SCANM, LF], writes=[CUM])
                        C3 = CUM.ap.rearrange("p (c s) -> p c s", s=64)
                        tt("vector", EQ.ap.rearrange("p (c s) -> p c s", s=64), C3[:, :, 63:64].to_broadcast([128, 8, 64]), C3,
                           ALU.subtract, [CUM], [EQ])
                        act(EQ.ap, EQ.ap, AF.Exp, [EQ], [EQ])
                        tt("vector", KSb.ap, KK.ap, EQ.ap, ALU.mult, [KK, EQ], [KSb])
                        act(CH[:, 1], C3[:, :, 63], AF.Exp, [CUM], [CH])
                        for c in range(8):
                            tt("gpsimd", DL[:, h:h + 1], DL[:, h:h + 1], CUM[:, c * 64 + 63:c * 64 + 64], ALU.add, [DL, CUM], [DL])
                        Sh = Tl(None, [S.bufs[h]])
                        for pr in range(4):
                            pb = self.psb_alloc()
                            tr(pb, pb[:, 0:128], KSb[:, pr * 128:(pr + 1) * 128], IDb.ap, [KSb, IDb])
                            cp("scalar", KST[:, pr], pb[:, 0:128], [pb], [KST])
                            self.psb_release(pb)
                            for cc in range(2):
                                c = pr * 2 + cc
                                pd = self.ps_alloc()
                                mm(pd, pd[:, 0:128], KST[cc * 64:(cc + 1) * 64, pr], VT[cc * 64:(cc + 1) * 64, pr, h * 128:(h + 1) * 128],
                                   True, True, [KST, VTt[pr]])
                                stt("vector", S[:, h], S[:, h], CH[:, 1, c:c + 1], pd[:, 0:128], ALU.mult, ALU.add, [Sh, CH, pd], [Sh])
                                self.ps_release(pd)
                if blk == self.nblk - 1:
                    wt = self.wpanel()
                    for g in range(2):
                        for rr in range(2):
                            load_cols(wt, g * 128 + rr * 64, wl, O_AK + g * 64, 64)
                            load_cols(wt, 256 + g * 128 + rr * 64, wl, O_AV + g * 64, 64)
                    for g in range(2):
                        def kev0(ps, g=g):
                            rms_fm(ps, BDb, 1.0 / 64)
                            stt("vector", KT[g][:, 4], ps[:, 384:512], SMALL[:, l, 2:3], RS[:, 384:512],
                                ALU.mult, ALU.mult, [ps, SMALL, RS], [Tl(None, [KT[g].bufs[4]])])
                        gemm_fm(wt, g * 128, kev0)
                    ps = self.ps_alloc()
                    for kc in range(16):
                        mm(ps, ps[:, 0:256], self.HT[:, kc, 384:512], wt[:, kc, 256:512], kc == 0, kc == 15, [wt, HTall])
                    cp(evac_eng(), V2[:, 4], ps[:, 0:256], [ps], [Tl(None, [V2.bufs[4]])])
                    self.ps_release(ps)
            dma("sync", cin[:, 0:1024], S2d, [S], [cin_t])
            dma("sync", cin[:, 1024:1032], DL.ap, [DL], [cin_t])
            for g in range(2):
                dma("gpsimd", cin[:, 1032 + g * 128:1032 + (g + 1) * 128], KT[g][:, 4], [Tl(None, [KT[g].bufs[4]])], [cin_t])
            dma("gpsimd", cin[:, 1288:1544], V2[:, 4], [Tl(None, [V2.bufs[4]])], [cin_t])
            p.op("gpsimd", lambda e: e.collective_compute("AllGather", ALU.bypass, replica_groups=[list(range(8))],
                                                          ins=[cin.opt()], outs=[cout.opt()]),
                 reads=[cin_t], writes=[cout_t], dma=True, inc=1)
            p.op("vector", lambda e: e.memset(S.ap, 0.0), writes=[S])
            for r in range(8):
                dma("sync", UR.ap, cout[r * 128:(r + 1) * 128, 0:1024], [cout_t], [UR])
                dma("sync", DLR.ap, cout[r * 128:(r + 1) * 128, 1024:1032], [cout_t], [DLR])
                act(DM.ap, DLR.ap, AF.Exp, [DLR], [DM])
                ts("vector", DM.ap, DM.ap, -1.0, CM[:, r:r + 1], ALU.add, ALU.mult, [DM, CM], [DM])
                p.op("vector", lambda e: e.tensor_scalar_add(DM.ap, DM.ap, 1.0), reads=[DM], writes=[DM])
                tt("vector", S.ap, S.ap, DM.ap.rearrange("p (h o) -> p h o", o=1).to_broadcast([128, 8, 128]), ALU.mult, [S, DM], [S])
                stt("vector", S2d, UR.ap, CM[:, r:r + 1], S2d, ALU.mult, ALU.add, [UR, CM, S], [S])
            cp("scalar", Sb.ap, S.ap, [S], [Sb])
            c3 = cout.rearrange("(r p) n -> p r n", p=128)
            dma("sync", KTL.ap, c3[:, :, 1032:1288], [cout_t], [KTL])
            dma("sync", VTL.ap, c3[:, :, 1288:1544], [cout_t], [VTL])
            p.op("vector", lambda e: e.memset(T1[:, 0:256], 0.0), writes=[T1])
            p.op("vector", lambda e: e.memset(T2[:, 0:256], 0.0), writes=[T2])
            for r in range(8):
                stt("vector", T1[:, 0:256], KTL[:, r], CM[:, 8 + r:9 + r], T1[:, 0:256], ALU.mult, ALU.add, [KTL, CM, T1], [T1])
                stt("vector", T2[:, 0:256], VTL[:, r], CM[:, 8 + r:9 + r], T2[:, 0:256], ALU.mult, ALU.add, [VTL, CM, T2], [T2])
            for g in range(2):
                cp("vector", KT[g][:, 0], T1[:, g * 128:(g + 1) * 128], [T1], [Tl(None, [KT[g].bufs[0]])])
            cp("vector", V2[:, 0], T2[:, 0:256], [T2], [Tl(None, [V2.bufs[0]])])
            for blk in range(self.nblk):
                r0 = blk * TB
                xsb = Tl(None, [self.xs_t.bufs[blk]])
                for t in range(4):
                    dma("sync", self.X[:, t], src[r0 + t * 128:r0 + (t + 1) * 128, :],
                        [xsb] if l == 1 else [], [Tl(None, [self.X.bufs[t]])])
                try:
                    norm(GMIX, l)
                    if self.stop == 1:
                        raise _Stop()
                    for hp in range(2):
                        wt = self.wpanel()
                        load_cols(wt, 0, wl, O_I + hp * 512, 512)
                        for t in range(4):
                            ps = self.ps_alloc()
                            for kc in range(16):
                                mm(ps, ps.ap, self.HT[:, kc, t * 128:(t + 1) * 128], wt[:, kc, :], kc == 0, kc == 15, [wt, HTall])
                            cp(evac_eng(), VT[:, t, hp * 512:(hp + 1) * 512], ps.ap, [ps], [VTt[t]])
                            self.ps_release(ps)
                    if self.stop == 2:
                        raise _Stop()
                    for h in range(8):
                        wt = self.wpanel()
                        load_cols(wt, 0, wl, O_Q + h * 128, 128)
                        load_cols(wt, 128, wl, O_F + h * 128, 128)
                        load_cols(wt, 256, wl, O_G + h * 128, 128)
                        gemm_fm(wt, 0, lambda ps: act(QS.ap, ps.ap, AF.Silu, [ps], [QS]))
                        gemm_fm(wt, 128, lambda ps: act(KK.ap, ps.ap, AF.Sigmoid, [ps], [KK], scale=-1.0))
                        tsm("vector", KK.ap, KK.ap, OML[:, l, h:h + 1], [KK, OML], [KK])
                        act(LF.ap, KK.ap, AF.Ln, [KK, ONE1], [LF], scale=-1.0, bias=ONE1[:, 0:1])
                        p.op("vector", lambda e: e.tensor_tensor_scan(CUM.ap, SCANM.ap, LF.ap, 0.0, ALU.mult, ALU.add),
                             reads=[SCANM, LF], writes=[CUM])
                        C3 = CUM.ap.rearrange("p (c s) -> p c s", s=64)
                        tt("vector", LF.ap.rearrange("p (c s) -> p c s", s=64), C3, C3[:, :, 31:32].to_broadcast([128, 8, 64]),
                           ALU.subtract, [CUM], [LF])
                        act(EQ.ap, LF.ap, AF.Exp, [LF], [EQ])
                        act(EK.ap, LF.ap, AF.Exp, [LF], [EK], scale=-1.0)
                        act(CH[:, 0], C3[:, :, 31], AF.Exp, [CUM], [CH])
                        act(CH[:, 1], C3[:, :, 63], AF.Exp, [CUM], [CH])
                        tt("vector", CH[:, 2], C3[:, :, 63], C3[:, :, 31], ALU.subtract, [CUM], [CH])
                        act(CH[:, 2], CH[:, 2], AF.Exp, [CH], [CH])
                        tt("vector", QAb.ap, QS.ap, EQ.ap, ALU.mult, [QS, EQ], [QAb])
                        tt("gpsimd", QOb.ap.rearrange("p (c s) -> p c s", s=64), QAb.ap.rearrange("p (c s) -> p c s", s=64),
                           CH[:, 0].rearrange("p (c o) -> p c o", o=1).to_broadcast([128, 8, 64]), ALU.mult, [QAb, CH], [QOb])
                        tt("vector", KAb.ap, KK.ap, EK.ap, ALU.mult, [KK, EK], [KAb])
                        tt("gpsimd", KSb.ap.rearrange("p (c s) -> p c s", s=64), KAb.ap.rearrange("p (c s) -> p c s", s=64),
                           CH[:, 2].rearrange("p (c o) -> p c o", o=1).to_broadcast([128, 8, 64]), ALU.mult, [KAb, CH], [KSb])
                        Sh = Tl(None, [S.bufs[h]]); Sbh = Tl(None, [Sb.bufs[h]])
                        po = self.ps_alloc()
                        for pr in range(4):
                            tsl = slice(pr * 128, (pr + 1) * 128)
                            pb = self.psb_alloc()
                            tr(pb, pb[:, 0:128], KSb[:, tsl], IDb.ap, [KSb, IDb])
                            cp("scalar", KST[:, pr], pb[:, 0:128], [pb], [KST])
                            self.psb_release(pb)
                            pa = self.ps_alloc()
                            mm(pa, pa[:, 0:128], KAb[:, tsl], QAb[:, tsl], True, True, [KAb, QAb])
                            tt("vector", AM[:, pr], pa[:, 0:128], AMb.ap, ALU.mult, [pa, AMb], [AM])
                            self.ps_release(pa)
                            for cc in range(2):
                                c = pr * 2 + cc
                                csl = slice(c * 64, (c + 1) * 64)
                                mm(po, po[:, csl], VT[:, pr, h * 128:(h + 1) * 128], AM[:, pr, cc * 64:(cc + 1) * 64],
                                   True, False, [VTt[pr], AM])
                                mm(po, po[:, csl], Sb[:, h], QOb[:, csl], False, True, [Sbh, QOb])
                                pd = self.ps_alloc()
                                mm(pd, pd[:, 0:128], KST[cc * 64:(cc + 1) * 64, pr], VT[cc * 64:(cc + 1) * 64, pr, h * 128:(h + 1) * 128],
                                   True, True, [KST, VTt[pr]])
                                stt("vector", S[:, h], S[:, h], CH[:, 1, c:c + 1], pd[:, 0:128], ALU.mult, ALU.add, [Sh, CH, pd], [Sh])
                                self.ps_release(pd)
                                cp("scalar", Sb[:, h], S[:, h], [Sh], [Sbh])
                        rms_fm(po, ONESb, 1.0 / 128)
                        stt("vector", T1.ap, po.ap, SMALL[:, l, 0:1], RS.ap, ALU.mult, ALU.mult, [po, SMALL, RS], [T1])
                        self.ps_release(po)
                        gemm_fm(wt, 256, lambda ps: act(T2.ap, ps.ap, AF.Silu, [ps], [T2]))
                        tt("vector", OHG[:, h], T1.ap, T2.ap, ALU.mult, [T1, T2], [OHGh[h]])
                    if self.stop == 3:
                        raise _Stop()
                    wt = self.wpanel()
                    for g in range(2):
                        for rr in range(2):
                            load_cols(wt, g * 128 + rr * 64, wl, O_AK + g * 64, 64)
                            load_cols(wt, 256 + g * 128 + rr * 64, wl, O_AV + g * 64, 64)
                    for g in range(2):
                        def kev(ps, g=g):
                            rms_fm(ps, BDb, 1.0 / 64)
                            for t in range(4):
                                stt("vector", KT[g][:, 1 + t], ps[:, t * 128:(t + 1) * 128], SMALL[:, l, 2:3], RS[:, t * 128:(t + 1) * 128],
                                    ALU.mult, ALU.mult, [ps, SMALL, RS], [Tl(None, [KT[g].bufs[1 + t]])])
                        gemm_fm(wt, g * 128, kev)
                    for t in range(4):
                        ps = self.ps_alloc()
                        for kc in range(16):
                            mm(ps, ps[:, 0:256], self.HT[:, kc, t * 128:(t + 1) * 128], wt[:, kc, 256:512], kc == 0, kc == 15, [wt, HTall])
                        cp(evac_eng(), V2[:, 1 + t], ps[:, 0:256], [ps], [Tl(None, [V2.bufs[1 + t]])])
                        self.ps_release(ps)
                    for qp in range(2):
                        wt = self.wpanel()
                        load_cols(wt, 0, wl, O_AQ + qp * 512, 512)
                        for jj in range(4):
                            j = qp * 4 + jj

                            def qev(ps, j=j):
                                rms_fm(ps, BDb, 1.0 / 64)
                                stt("vector", QH[:, j], ps.ap, QN2S[:, l:l + 1], RS.ap, ALU.mult, ALU.mult, [ps, QN2S, RS], [QHj[j]])
                            gemm_fm(wt, jj * 128, qev)
                    if self.stop == 4:
                        raise _Stop()
                    pi = 0
                    for g in range(2):
                        QHg = Tl(None, [b for j in range(4 * g, 4 * g + 4) for b in QHj[j].bufs])
                        for rr in range(2):
                            rs_ = slice(rr * 64, (rr + 1) * 64)
                            bias_t = BIAS[g * 2 + rr]
                            B4 = bias_t.ap.rearrange("p (k n) -> p k n", k=2)
                            for t in range(4):
                                Pt = PT[pi % 2]
                                pi += 1
                                for kt in range(2):
                                    slot = t + kt
                                    ps = self.ps_alloc()
                                    mm(ps, ps.ap, KT[g][rs_, slot], QH[rs_, 4 * g:4 * g + 4, t * 128:(t + 1) * 128], True, True,
                                       [Tl(None, [KT[g].bufs[slot]]), QHg])
                                    if blk == 0 and t == 0 and kt == 0:
                                        stt("vector", T1.ap, ps.ap, HALO[:, 0:1], B4[:, kt], ALU.add, ALU.add, [ps, HALO, bias_t], [T1])
                                    else:
                                        tt("vector", T1.ap, ps.ap, B4[:, kt], ALU.add, [ps, bias_t], [T1])
                                    self.ps_release(ps)
                                    act(Pt[:, kt], T1.ap, AF.Exp, [T1], [Pt])
                                pn = self.ps_alloc()
                                pdn = self.ps_alloc()
                                for kt in range(2):
                                    slot = t + kt
                                    mm(pn, pn.ap, V2[:, slot, g * 128:(g + 1) * 128], Pt[:, kt], kt == 0, kt == 1,
                                       [Tl(None, [V2.bufs[slot]]), Pt])
                                for kt in range(2):
                                    mm(pdn, pdn.ap, ONESb.ap, Pt[:, kt], kt == 0, kt == 1, [ONESb, Pt])
                                d3 = DEN[rs_, :].rearrange("p (j q) -> p j q", j=4)
                                tt("vector", d3, pdn[rs_, :].rearrange("p (j q) -> p j q", j=4),
                                   ESINK[rs_, l, 4 * g:4 * g + 4].rearrange("p (j o) -> p j o", o=1).to_broadcast([64, 4, 128]),
                                   ALU.add, [pdn, ESINK], [DEN])
                                self.ps_release(pdn)
                                p.op("vector", lambda e, rs_=rs_: e.reciprocal(DEN[rs_, :], DEN[rs_, :]), reads=[DEN], writes=[DEN])
                                tt("vector", ATT[rs_, 4 * g:4 * g + 4, t * 128:(t + 1) * 128], pn[rs_, :].rearrange("p (j q) -> p j q", j=4),
                                   d3, ALU.mult, [pn, DEN], [Tl(None, [b for j in range(4 * g, 4 * g + 4) for b in ATTj[j].bufs])])
                                self.ps_release(pn)
                    if self.stop == 5:
                        raise _Stop()
                    for g in range(2):
                        cp("gpsimd", KT[g][:, 0], KT[g][:, 4], [Tl(None, [KT[g].bufs[4]])], [Tl(None, [KT[g].bufs[0]])])
                    cp("gpsimd", V2[:, 0], V2[:, 4], [Tl(None, [V2.bufs[4]])], [Tl(None, [V2.bufs[0]])])
                    OHGall = Tl(None, [b for t_ in OHGh for b in t_.bufs])
                    ATTall = Tl(None, [b for t_ in ATTj for b in t_.bufs])
                    for jj in range(4):
                        wt = self.wpanel()
                        load_cols(wt, 0, wl, O_GH + jj * 512, 512)
                        for c in range(4):
                            gemm_fm(wt, c * 128, lambda ps, c=c: act(SG1[:, c], ps.ap, AF.Sigmoid, [ps], [SG1c[c]]))
                        wt = self.wpanel()
                        load_cols(wt, 0, wl, O_GA + jj * 512, 512)
                        for c in range(4):
                            gemm_fm(wt, c * 128, lambda ps, c=c: act(SGA[:, c], ps.ap, AF.Sigmoid, [ps], [SGAc[c]]))
                        wt = self.wpanel()
                        dma("gpsimd", wt[:, 0:8, :], w_hg[l].rearrange("(kc p) n -> p kc n", p=128)[:, :, jj * 512:(jj + 1) * 512], [WHG_T], [wt])
                        dma("gpsimd", wt[:, 8:16, :], w_at[l].rearrange("(kc p) n -> p kc n", p=128)[:, :, jj * 512:(jj + 1) * 512], [WAT_T], [wt])
                        for c in range(4):
                            j16 = jj * 4 + c
                            ps = self.ps_alloc()
                            for kc in range(8):
                                mm(ps, ps.ap, wt[:, kc, c * 128:(c + 1) * 128], OHG[:, kc], kc == 0, kc == 7, [wt, OHGall])
                            tt("vector", T1.ap, ps.ap, SG1[:, c], ALU.mult, [ps, SG1c[c]], [T1])
                            self.ps_release(ps)
                            ps = self.ps_alloc()
                            for kc in range(8):
                                mm(ps, ps.ap, wt[:, 8 + kc, c * 128:(c + 1) * 128], ATT[:, kc], kc == 0, kc == 7, [wt, ATTall])
                            tt("vector", T2.ap, ps.ap, SGA[:, c], ALU.mult, [ps, SGAc[c]], [T2])
                            self.ps_release(ps)
                            tt("gpsimd", MIX[:, j16], T1.ap, T2.ap, ALU.add, [T1, T2], [MIXc[j16]])
                    if self.stop == 6:
                        raise _Stop()
                    MIXall = Tl(None, [b for t_ in MIXc for b in t_.bufs])
                    for cg in range(4):
                        wt = self.wpanel()
                        load_cols(wt, 0, w_out[l], cg * 512, 512, WOUT_T)
                        for t in range(4):
                            ps = self.ps_alloc()
                            for kc in range(16):
                                mm(ps, ps.ap, MIX[:, kc, t * 128:(t + 1) * 128], wt[:, kc, :], kc == 0, kc == 15, [wt, MIXall])
                            xt = Tl(None, [self.X.bufs[t]])
                            tt("vector", self.X[:, t, cg * 512:(cg + 1) * 512], self.X[:, t, cg * 512:(cg + 1) * 512], ps.ap, ALU.add, [ps, xt], [xt])
                            self.ps_release(ps)
                    if self.stop == 7:
                        raise _Stop()
                    norm(GFFN, l)
                    for fg in range(16):
                        wt = self.wpanel()
                        load_cols(wt, 0, w_up[l], fg * 512, 512, WUP_T)
                        for c in range(4):
                            fc = fg * 4 + c

                            def uev(ps, fc=fc):
                                act(T1.ap, ps.ap, AF.Relu, [ps], [T1])
                                tt("gpsimd" if fc % 2 else "vector", AF_[:, fc], T1.ap, T1.ap, ALU.mult, [T1], [AFc[fc]])
                            gemm_fm(wt, c * 128, uev)
                    if self.stop == 8:
                        raise _Stop()
                    for cg in range(4):
                        pss = [self.ps_alloc() for _ in range(4)]
                        for fg in range(4):
                            wt = self.wpanel()
                            srcw = w_dn[l][fg * 2048:(fg + 1) * 2048, :].rearrange("(f p) n -> p f n", p=128)[:, :, cg * 512:(cg + 1) * 512]
                            dma("gpsimd", wt.ap, srcw, [WDN_T], [wt])
                            for c in range(4):
                                for f in range(16):
                                    fc = fg * 16 + f
                                    mm(pss[c], pss[c].ap, wt[:, f, c * 128:(c + 1) * 128], AF_[:, fc], fg == 0 and f == 0, fg == 3 and f == 15,
                                       [wt, AFc[fc]])
                        for c in range(4):
                            yt = YT[(cg % 2) * 4 + c]
                            cp(evac_eng(), yt.ap, pss[c].ap, [pss[c]], [yt])
                            self.ps_release(pss[c])
                        for t in range(4):
                            pt = self.ps_alloc()
                            for c in range(4):
                                yt = YT[(cg % 2) * 4 + c]
                                tr(pt, pt[:, c * 128:(c + 1) * 128], yt[:, t * 128:(t + 1) * 128], IDf.ap, [yt, IDf])
                            xt = Tl(None, [self.X.bufs[t]])
                            tt("vector", self.X[:, t, cg * 512:(cg + 1) * 512], self.X[:, t, cg * 512:(cg + 1) * 512], pt.ap, ALU.add, [pt, xt], [xt])
                            self.ps_release(pt)
                except _Stop:
                    pass
                for t in range(4):
                    dma("sync", dst[r0 + t * 128:r0 + (t + 1) * 128, :], self.X[:, t], [Tl(None, [self.X.bufs[t]])],
                        [xsb] if l == 0 else [], final=(l == L - 1))
        self.p.emit()
        self.st.close()
        return self.nc


_CACHE = {}


def _layout_small(norm_mix, norm_ffn, lb_logits, hg_norm, q_norm, k_norm, sinks):
    f = np.float32
    gmix = np.ascontiguousarray(norm_mix.reshape(L, 16, 128).transpose(0, 2, 1)).astype(f)
    gffn = np.ascontiguousarray(norm_ffn.reshape(L, 16, 128).transpose(0, 2, 1)).astype(f)
    lbl = np.ascontiguousarray(lb_logits.reshape(L, 8, 128).transpose(0, 2, 1)).astype(f)
    small = np.zeros((L, 128, 16), f)
    small[:, :, 0] = hg_norm
    small[:, :, 1] = np.concatenate([q_norm, q_norm], axis=1)
    small[:, :, 2] = np.concatenate([k_norm, k_norm], axis=1)
    for j in range(8):
        small[:, 0:64, 8 + j] = sinks[:, 2 * j][:, None]
        small[:, 64:128, 8 + j] = sinks[:, 2 * j + 1][:, None]
    return gmix, gffn, lbl, small


NSEG = 4


def run(x, norm_mix, w_in, lb_logits, hg_norm, q_norm, k_norm, sinks, w_hg_out, w_at_out, w_out,
        norm_ffn, w_up, w_down, trace=False):
    x = np.asarray(x, np.float32)
    nseq, T, _ = x.shape
    assert nseq * NSEG == 8
    Tc = T // NSEG
    nblk = Tc // TB
    if nblk not in _CACHE:
        _CACHE[nblk] = Builder(nblk).build()
    nc = _CACHE[nblk]
    mats, scanmask, bias = host_consts()
    gmix, gffn, lbl, small = _layout_small(*[np.asarray(a, np.float32) for a in
                                             (norm_mix, norm_ffn, lb_logits, hg_norm, q_norm, k_norm, sinks)])
    base = dict(gmix=gmix, gffn=gffn, lbl=lbl, small=small, cmats=mats, cscan=scanmask, cbias=bias)
    wts = dict(w_in=w_in, w_hg_out=w_hg_out, w_at_out=w_at_out, w_out=w_out, w_up=w_up, w_down=w_down)
    wsh = {}
    for k, v in wts.items():
        v = np.asarray(v, np.float32)
        v2 = v.reshape(v.shape[0] * v.shape[1], v.shape[2])
        wsh[k] = np.split(v2, 8, axis=0)
    in_maps = []
    for c in range(8):
        b, sg = c // NSEG, c % NSEG
        halo = np.zeros((128, 8), np.float32)
        halo[:, 0] = NEG if sg == 0 else 0.0
        cm = np.zeros((128, 16), np.float32)
        for r in range(8):
            if r // NSEG == b and r % NSEG < sg:
                cm[:, r] = 1.0
            if r // NSEG == b and r == c - 1:
                cm[:, 8 + r] = 1.0
        m = dict(base, x=np.ascontiguousarray(x[b, sg * Tc:(sg + 1) * Tc]), halo=halo, cmask=cm)
        for k in wsh:
            m[k] = np.ascontiguousarray(wsh[k][c])
        in_maps.append(m)
    res = run_bass_kernel_spmd(nc, in_maps, core_ids=list(range(8)), **({"trace": True} if trace else {}))
    out = np.stack([np.concatenate([np.asarray(res.results[b * NSEG + sg]["y"]) for sg in range(NSEG)], 0)
                    for b in range(nseq)], 0)
    return out, res


def kernel(x, norm_mix, w_in, lb_logits, hg_norm, q_norm, k_norm, sinks, w_hg_out, w_at_out, w_out,
           norm_ffn, w_up, w_down):
    out, _ = run(x, norm_mix, w_in, lb_logits, hg_norm, q_norm, k_norm, sinks, w_hg_out, w_at_out, w_out,
                 norm_ffn, w_up, w_down)
    return out.astype(np.float32)
```

```python
import contextlib
import numpy as np
import concourse.bass as bass
import concourse.mybir as mybir
from concourse.bass_utils import run_bass_kernel_spmd

F32 = mybir.dt.float32
BF16 = mybir.dt.bfloat16
AF = mybir.ActivationFunctionType
ALU = mybir.AluOpType

D = 2048
DIN = 9472
DFF = 8192
L = 2
TB = 512
EPS = 1e-6
O_Q, O_F, O_I, O_G, O_AQ, O_AK, O_AV, O_GH, O_GA = 0, 1024, 2048, 3072, 4096, 5120, 5248, 5376, 7424
NEG = -30000.0

ENGS = ("tensor", "vector", "scalar", "gpsimd", "sync")


class Buf:
    __slots__ = ("name", "w", "r")

    def __init__(self, name):
        self.name = name
        self.w = None
        self.r = {}


class Tl:
    __slots__ = ("ap", "bufs")

    def __init__(self, ap, bufs):
        self.ap = ap
        self.bufs = bufs

    def __getitem__(self, k):
        return self.ap[k]


class Op:
    __slots__ = ("eng", "fn", "waits", "idx", "dma", "dsem", "dval")


class Prog:
    def __init__(self, nc, n_dma_sems=8):
        self.nc = nc
        self.ops = {e: [] for e in ENGS}
        self.cnt = {e: 0 for e in ENGS}
        self.needed = {e: set() for e in ENGS}
        self.waited = {e: {} for e in ENGS}
        self.n_dma_sems = n_dma_sems
        self.dma_rr = {e: 0 for e in ENGS}
        self.dma_val = {}
        self.final_dma = []

    def _add_wait(self, op, ev):
        if ev is None:
            return
        kind, key, val = ev
        if kind == "e" and key == op.eng and key == "tensor":
            return
        w = self.waited[op.eng]
        k = (kind, key)
        if w.get(k, 0) >= val:
            return
        w[k] = val
        op.waits.append(ev)
        if kind == "e":
            self.needed[key].add(val)

    def op(self, eng, fn, reads=(), writes=(), dma=False, final=False, inc=16):
        o = Op()
        o.eng = eng
        o.fn = fn
        o.waits = []
        o.dma = dma
        rb = [b for t in reads for b in t.bufs]
        wb = [b for t in writes for b in t.bufs]
        for b in rb:
            self._add_wait(o, b.w)
        for b in wb:
            self._add_wait(o, b.w)
            for (kk, ky), vv in b.r.items():
                self._add_wait(o, (kk, ky, vv))
        if dma:
            k = self.dma_rr[eng]
            self.dma_rr[eng] = (k + 1) % self.n_dma_sems
            key = (eng, k)
            prev = self.dma_val.get(key, 0)
            if prev:
                self._add_wait(o, ("d", key, prev))
            self.dma_val[key] = prev + inc
            o.dsem = key
            o.dval = inc
            ev = ("d", key, prev + inc)
            o.idx = None
            if final:
                self.final_dma.append(ev)
        else:
            self.cnt[eng] += 1
            o.idx = self.cnt[eng]
            ev = ("e", eng, o.idx)
        for b in rb:
            b.r[(ev[0], ev[1])] = ev[2]
        for b in wb:
            b.w = ev
            b.r = {}
        self.ops[eng].append(o)
        return ev

    def emit(self):
        nc = self.nc
        EP = 30000
        remap = {}
        nep = {}
        for e in ENGS:
            remap[e] = {i: (n // EP, n % EP + 1) for n, i in enumerate(sorted(self.needed[e]))}
            nep[e] = max(1, (len(self.needed[e]) + EP - 1) // EP)
        fin = Op()
        fin.eng = "sync"; fin.fn = None; fin.waits = []; fin.dma = False; fin.idx = None
        for ev in self.final_dma:
            self._add_wait(fin, ev)
        self.ops["sync"].append(fin)
        with contextlib.ExitStack() as st:
            esem = {e: [st.enter_context(nc.semaphore("es_%s%d" % (e, k))) for k in range(nep[e])] for e in ENGS}
            dsem = {}
            for key in self.dma_val:
                dsem[key] = st.enter_context(nc.semaphore("ds_%s%d" % key))
            block = st.enter_context(nc.Block())

            def run(eng_name, eng):
                for o in self.ops[eng_name]:
                    for kind, key, val in o.waits:
                        if kind == "e":
                            ep, v = remap[key][val]
                            eng.wait_ge(esem[key][ep], v)
                        else:
                            eng.wait_ge(dsem[key], val)
                    if o.fn is None:
                        continue
                    ins = o.fn(eng)
                    if o.dma:
                        ins.then_inc(dsem[o.dsem], o.dval)
                    elif o.idx in remap[eng_name]:
                        ins.then_inc(esem[eng_name][remap[eng_name][o.idx][0]], 1)

            @block.tensor
            def _(eng):
                run("tensor", eng)

            @block.vector
            def _(eng):
                run("vector", eng)

            @block.scalar
            def _(eng):
                run("scalar", eng)

            @block.gpsimd
            def _(eng):
                run("gpsimd", eng)

            @block.sync
            def _(eng):
                run("sync", eng)


def host_consts():
    f = np.float32
    ident = np.eye(128, dtype=f)
    ones = np.ones((128, 128), f)
    bdiag = np.kron(np.eye(2, dtype=f), np.ones((64, 64), f))
    s = np.arange(128)[:, None]
    c = np.arange(128)[None, :]
    amask = ((s // 64 == c // 64) & (s <= c)).astype(f)
    mats = np.stack([ident, ones, bdiag, amask], 0)
    scanmask = np.ones((128, 512), f)
    scanmask[:, ::64] = 0
    slopes = (2.0 ** (-8.0 * (np.arange(16, dtype=np.float32) + 1.0) / 16)).astype(f)
    k = np.arange(128)[:, None, None]
    q = np.arange(128)[None, None, :]
    bias = np.zeros((2, 2, 128, 2, 4, 128), f)
    for g in range(2):
        for r in range(2):
            for j in range(4):
                h = g * 8 + 2 * j + r
                d0 = (q + 128 - k).astype(f)[:, 0, :]
                v0 = (k > q)[:, 0, :]
                d1 = (q - k).astype(f)[:, 0, :]
                v1 = (q >= k)[:, 0, :]
                bias[g, r, :, 0, j, :] = np.where(v0, -slopes[h] * d0, NEG)
                bias[g, r, :, 1, j, :] = np.where(v1, -slopes[h] * d1, NEG)
    return mats, scanmask, bias.reshape(4, 128, 1024)


class _Stop(Exception):
    pass


STOP = [99]


class Builder:
    def __init__(self, nblk):
        self.nblk = nblk
        self.stop = STOP[0]
        self.T = nblk * TB
        self.nc = bass.Bass("TRN2", target_bir_lowering=False)
        self.p = Prog(self.nc)
        self.st = contextlib.ExitStack()
        self.off = 0

    def sb(self, name, shape, dt, nbuf=1):
        esz = 4 if dt == F32 else 2
        n = int(np.prod(shape[1:]))
        sz = n * esz
        a = self.ARENA[:, self.off // 2:(self.off + sz) // 2]
        self.off += (sz + 63) // 64 * 64
        assert self.off <= self.arena_bytes, (name, self.off)
        if dt == F32:
            a = a.bitcast(F32)
        if len(shape) == 3:
            a = a.rearrange("p (a b) -> p a b", b=shape[2])
        return Tl(a, [Buf(name + str(i)) for i in range(nbuf)])

    def view(self, bufs, byte_off, shape, dt):
        esz = 4 if dt == F32 else 2
        n = int(np.prod(shape[1:]))
        a = self.ARENA[:, byte_off // 2:(byte_off + n * esz) // 2]
        if dt == F32:
            a = a.bitcast(F32)
        if len(shape) == 3:
            a = a.rearrange("p (a b) -> p a b", b=shape[2])
        return Tl(a, bufs)

    def ps_alloc(self):
        assert self.ps_free, "out of PSUM banks"
        return self.ps_free.pop(0)

    def ps_release(self, t):
        self.ps_free.append(t)

    def psb_alloc(self):
        assert self.psb_free
        return self.psb_free.pop(0)

    def psb_release(self, t):
        self.psb_free.append(t)

    def mm(self, out_t, out_ap, lhsT, rhs, start, stop, reads):
        self.p.op("tensor", lambda e: e.matmul(out_ap, lhsT, rhs, start=start, stop=stop),
                  reads=reads, writes=[out_t])

    def tr(self, out_t, out_ap, in_ap, ident_ap, reads):
        self.p.op("tensor", lambda e: e.transpose(out_ap, in_ap, ident_ap), reads=reads, writes=[out_t])

    def act(self, out_ap, in_ap, func, reads, writes, scale=None, bias=None, accum=None, eng="scalar"):
        kw = {}
        if scale is not None:
            kw["scale"] = scale
        if bias is not None:
            kw["bias"] = bias
        if accum is not None:
            kw["accum_out"] = accum
        self.p.op("scalar", lambda e: e.activation(out=out_ap, in_=in_ap, func=func, **kw),
                  reads=reads, writes=writes)

    def tt(self, eng, out_ap, in0, in1, op, reads, writes):
        self.p.op(eng, lambda e: e.tensor_tensor(out_ap, in0, in1, op), reads=reads, writes=writes)

    def ts(self, eng, out_ap, in0, s1, s2, op0, op1, reads, writes):
        self.p.op(eng, lambda e: e.tensor_scalar(out_ap, in0, s1, s2, op0, op1), reads=reads, writes=writes)

    def tsm(self, eng, out_ap, in0, s1, reads, writes):
        self.p.op(eng, lambda e: e.tensor_scalar_mul(out_ap, in0, s1), reads=reads, writes=writes)

    def stt(self, eng, out_ap, in0, sc, in1, op0, op1, reads, writes):
        self.p.op(eng, lambda e: e.scalar_tensor_tensor(out_ap, in0, sc, in1, op0, op1), reads=reads, writes=writes)

    def cp(self, eng, out_ap, in_ap, reads, writes):
        if eng == "scalar":
            self.p.op(eng, lambda e: e.activation(out=out_ap, in_=in_ap, func=AF.Copy), reads=reads, writes=writes)
        else:
            self.p.op(eng, lambda e: e.tensor_copy(out_ap, in_ap), reads=reads, writes=writes)

    def dma(self, eng, out_ap, in_ap, reads, writes, final=False):
        self.p.op(eng, lambda e: e.dma_start(out=out_ap, in_=in_ap), reads=reads, writes=writes, dma=True, final=final)

    def wpanel(self):
        t = self.WP[self.wp_i]
        self.wp_i = (self.wp_i + 1) % len(self.WP)
        return t

    def build(self):
        nc, st, T = self.nc, self.st, self.T
        dr = lambda n, s, k: nc.dram_tensor(n, s, F32, kind=k).ap()
        x = dr("x", [T, D], "ExternalInput")
        self.wgather = []

        def wshard(name, rows, cols):
            ext = dr(name, [rows // 8, cols], "ExternalInput")
            bnc = nc.dram_tensor(name + "_b", [rows // 8, cols], BF16).ap()
            full = nc.dram_tensor(name + "_f", [rows, cols], BF16).ap()
            bt = Tl(None, [Buf(name + "_b")]); ft = Tl(None, [Buf(name + "_f")])
            self.wgather.append((ext, bnc, full, bt, ft))
            per = rows // L
            return [full[l * per:(l + 1) * per, :] for l in range(L)], ft
        w_in, WIN_T = wshard("w_in", L * D, DIN)
        w_hg, WHG_T = wshard("w_hg_out", L * 1024, D)
        w_at, WAT_T = wshard("w_at_out", L * 1024, D)
        w_out, WOUT_T = wshard("w_out", L * D, D)
        w_up, WUP_T = wshard("w_up", L * D, DFF)
        w_dn, WDN_T = wshard("w_down", L * DFF, D)
        gmix = dr("gmix", [L, 128, 16], "ExternalInput")
        gffn = dr("gffn", [L, 128, 16], "ExternalInput")
        lbl = dr("lbl", [L, 128, 8], "ExternalInput")
        small = dr("small", [L, 128, 16], "ExternalInput")
        cmats = dr("cmats", [4, 128, 128], "ExternalInput")
        cscan = dr("cscan", [128, 512], "ExternalInput")
        cbias = dr("cbias", [4, 128, 1024], "ExternalInput")
        halo = dr("halo", [128, 8], "ExternalInput")
        cmask = dr("cmask", [128, 16], "ExternalInput")
        y = dr("y", [T, D], "ExternalOutput")
        PW = 1544
        cin = nc.dram_tensor("cin", [128, PW], F32).ap()
        cout = nc.dram_tensor("cout", [8 * 128, PW], F32).ap()
        cin_t = Tl(cin, [Buf("cin")]); cout_t = Tl(cout, [Buf("cout")])
        xs = nc.dram_tensor("xs", [T, D], F32).ap()
        self.xs_t = Tl(xs, [Buf("xs%d" % i) for i in range(self.nblk)])

        self.arena_bytes = 206 * 1024
        self.ARENA = st.enter_context(nc.sbuf_tensor("arena", [128, self.arena_bytes // 2], BF16))
        sb = self.sb
        self.X = sb("X", [128, 4, D], F32, 4)
        self.HT = sb("HT", [128, 16, TB], BF16, 16)
        ht_off = self.off - 16 * 1024
        self.WP = [sb("WP%d" % i, [128, 16, 512], BF16) for i in range(3)]
        self.wp_i = 0
        scr_off = self.off
        self.SCRB = [Buf("scr%d" % i) for i in range(64)]
        self.off += 64 * 1024
        self.scr_off = scr_off
        IDb = sb("IDb", [128, 128], BF16); ONESb = sb("ONESb", [128, 128], BF16)
        BDb = sb("BDb", [128, 128], BF16); AMb = sb("AMb", [128, 128], BF16)
        IDf = sb("IDf", [128, 128], F32)
        SCANM = sb("SCANM", [128, 512], F32)
        BIAS = [sb("BIAS%d" % i, [128, 1024], F32) for i in range(4)]
        HALO = sb("HALO", [128, 8], F32)
        EPST = sb("EPST", [128, 1], F32)
        ONE1 = sb("ONE1", [128, 1], F32)
        GMIX = sb("GMIX", [128, 2, 16], F32); GFFN = sb("GFFN", [128, 2, 16], F32)
        LBL = sb("LBL", [128, 2, 8], F32); OML = sb("OML", [128, 2, 8], F32)
        SMALL = sb("SMALL", [128, 2, 16], F32)
        QN2S = sb("QN2S", [128, 2], F32); ESINK = sb("ESINK", [128, 2, 8], F32)
        S = sb("S", [128, 8, 128], F32, 8); Sb = sb("Sb", [128, 8, 128], BF16, 8)
        KT = [sb("KT%d" % g, [128, 5, 128], BF16, 5) for g in range(2)]
        V2 = sb("V2", [128, 5, 256], BF16, 5)
        SSQ = sb("SSQ", [128, 8], F32); RSTD = sb("RSTD", [128, 8], F32)
        T1 = sb("T1", [128, 512], F32); T2 = sb("T2", [128, 512], F32)
        T1B = sb("T1B", [128, 512], F32)
        RS = sb("RS", [128, 512], F32); SQ = sb("SQ", [128, 512], BF16)
        CH = sb("CH", [128, 3, 8], F32)
        DEN = sb("DEN", [128, 512], F32)
        CM = sb("CM", [128, 16], F32); DL = sb("DL", [128, 8], F32)
        DLR = sb("DLR", [128, 8], F32); DM = sb("DM", [128, 8], F32)
        print("sbuf used KB", self.off / 1024)

        def scr(kb0, shape, dt, nb=None):
            esz = 4 if dt == F32 else 2
            nbytes = int(np.prod(shape[1:])) * esz
            k1 = kb0 + (nbytes + 1023) // 1024
            return self.view(self.SCRB[kb0:k1], scr_off + kb0 * 1024, shape, dt)

        def scr_split(kb0, shape, dt, n):
            esz = 4 if dt == F32 else 2
            per = int(np.prod(shape[2:])) * esz
            assert per % 1024 == 0
            kb = per // 1024
            full = self.view(self.SCRB[kb0:kb0 + n * kb], scr_off + kb0 * 1024, shape, dt)
            return full, [Tl(full.ap[:, i], self.SCRB[kb0 + i * kb: kb0 + (i + 1) * kb]) for i in range(n)]

        XN, XNt = scr_split(0, [128, 4, D], BF16, 4)
        VT, VTt = scr_split(0, [128, 4, 1024], BF16, 4)
        OHG, OHGh = scr_split(8, [128, 8, TB], BF16, 8)
        QH, QHj = scr_split(16, [128, 8, TB], BF16, 8)
        ATT, ATTj = scr_split(24, [128, 8, TB], BF16, 8)
        PT = [scr(32 + 4 * i, [128, 2, 512], BF16) for i in range(2)]
        QS = scr(44, [128, 512], F32); KK = scr(46, [128, 512], F32)
        LF = scr(48, [128, 512], F32); CUM = scr(50, [128, 512], F32)
        EQ = scr(52, [128, 512], F32); EK = scr(54, [128, 512], F32)
        QAb = scr(56, [128, 512], BF16); QOb = scr(57, [128, 512], BF16)
        KAb = scr(58, [128, 512], BF16); KSb = scr(59, [128, 512], BF16)
        KST = scr(60, [128, 4, 128], BF16); AM = scr(61, [128, 4, 128], BF16)
        SG1, SG1c = scr_split(40, [128, 4, TB], BF16, 4)
        SG2, SG2c = scr_split(62, [128, 2, TB], BF16, 2)
        MIX, MIXc = scr_split(44, [128, 16, TB], BF16, 16)
        UR = scr(44, [128, 1024], F32)
        KTL = scr(48, [128, 8, 256], F32); VTL = scr(56, [128, 8, 256], F32)
        SGA, SGAc = scr_split(32, [128, 4, TB], BF16, 4)
        AF_, AFc = scr_split(0, [128, 64, TB], BF16, 64)
        YT = [self.view(self.HT.bufs[2 * i:2 * i + 2], ht_off + 2048 * i, [128, 512], F32) for i in range(8)]

        self.ps_free = []
        for i in range(6):
            h = st.enter_context(nc.psum_tensor("ps%d" % i, [128, 512], F32))
            self.ps_free.append(Tl(h[:], [Buf("ps%d" % i)]))
        self.psb_free = []
        for i in range(2):
            hb = st.enter_context(nc.psum_tensor("psb%d" % i, [128, 1024], BF16))
            self.psb_free.append(Tl(hb[:, 0:512], [Buf("psb%d" % i)]))

        dma, act, tt, ts, tsm, stt, cp, mm, tr = self.dma, self.act, self.tt, self.ts, self.tsm, self.stt, self.cp, self.mm, self.tr
        p = self.p
        for ext, bnc, full, bt, ft in self.wgather:
            nr = ext.shape[0]
            for r0_ in range(0, nr, 128):
                dma("gpsimd", bnc[r0_:r0_ + 128, :], ext[r0_:r0_ + 128, :], [], [bt])
            p.op("gpsimd", lambda e, bnc=bnc, full=full: e.collective_compute(
                "AllGather", ALU.bypass, replica_groups=[list(range(8))], ins=[bnc.opt()], outs=[full.opt()]),
                reads=[bt], writes=[ft], dma=True, inc=1)
        dma("gpsimd", IDb.ap, cmats[0], [], [IDb]); dma("gpsimd", ONESb.ap, cmats[1], [], [ONESb])
        dma("gpsimd", BDb.ap, cmats[2], [], [BDb]); dma("gpsimd", AMb.ap, cmats[3], [], [AMb])
        dma("sync", IDf.ap, cmats[0], [], [IDf]); dma("sync", SCANM.ap, cscan, [], [SCANM])
        for i in range(4):
            dma("sync", BIAS[i].ap, cbias[i], [], [BIAS[i]])
        dma("sync", HALO.ap, halo, [], [HALO])
        dma("sync", CM.ap, cmask, [], [CM])
        for l in range(L):
            dma("sync", GMIX[:, l], gmix[l], [], [GMIX]); dma("sync", GFFN[:, l], gffn[l], [], [GFFN])
            dma("sync", LBL[:, l], lbl[l], [], [LBL]); dma("sync", SMALL[:, l], small[l], [], [SMALL])
        p.op("vector", lambda e: e.memset(EPST.ap, EPS), writes=[EPST])
        p.op("vector", lambda e: e.memset(ONE1.ap, 1.0), writes=[ONE1])
        p.op("vector", lambda e: e.memset(OML[:, 0], 1.0), writes=[OML])
        tt("vector", OML[:, 1], LBL[:, 0], LBL[:, 1], ALU.subtract, [LBL], [OML])
        act(OML[:, 1], OML[:, 1], AF.Sigmoid, [OML], [OML])
        for l in range(L):
            tsm("vector", QN2S[:, l:l + 1], SMALL[:, l, 1:2], 0.125, [SMALL], [QN2S])
            act(ESINK[:, l], SMALL[:, l, 8:16], AF.Exp, [SMALL], [ESINK])

        ev_alt = [0]

        def evac_eng():
            ev_alt[0] ^= 1
            return "vector" if ev_alt[0] else "scalar"

        def norm(g_t, l):
            for t in range(4):
                act(XNt[t].ap, self.X[:, t], AF.Square, [Tl(None, [self.X.bufs[t]])], [XNt[t], SSQ], accum=SSQ[:, t:t + 1])
            act(RSTD[:, 0:4], SSQ[:, 0:4], AF.Ln, [SSQ, EPST], [RSTD], scale=1.0 / D, bias=EPST[:, 0:1])
            act(RSTD[:, 0:4], RSTD[:, 0:4], AF.Exp, [RSTD], [RSTD], scale=-0.5)
            for t in range(4):
                eng = "vector" if t % 2 == 0 else "gpsimd"
                tsm(eng, XNt[t].ap, self.X[:, t], RSTD[:, t:t + 1], [Tl(None, [self.X.bufs[t]]), RSTD], [XNt[t]])
            for kc in range(16):
                pb = self.psb_alloc()
                for t in range(4):
                    tr(pb, pb[:, t * 128:(t + 1) * 128], XN[:, t, kc * 128:(kc + 1) * 128], IDb.ap, [XNt[t], IDb])
                hk = Tl(None, [self.HT.bufs[kc]])
                if kc % 2 == 0:
                    tsm("vector", self.HT[:, kc], pb.ap, g_t[:, l, kc:kc + 1], [pb, g_t], [hk])
                else:
                    act(self.HT[:, kc], pb.ap, AF.Copy, [pb, g_t], [hk], scale=g_t[:, l, kc:kc + 1])
                self.psb_release(pb)

        def load_cols(wt, dst_col, src2d, c0, ncols, dep=None):
            src = src2d.rearrange("(kc p) n -> p kc n", p=128)[:, :, c0:c0 + ncols]
            dma("sync", wt[:, :, dst_col:dst_col + ncols], src, [WIN_T if dep is None else dep], [wt])

        HTall = Tl(None, self.HT.bufs)

        def gemm_fm(wt, col0, evac, nk=16, rhs_fn=None, rhs_reads=None):
            ps = self.ps_alloc()
            for kc in range(nk):
                rhs = self.HT[:, kc] if rhs_fn is None else rhs_fn(kc)
                mm(ps, ps.ap, wt[:, kc, col0:col0 + 128], rhs, kc == 0, kc == nk - 1,
                   [wt] + ([HTall] if rhs_reads is None else rhs_reads))
            evac(ps)
            self.ps_release(ps)

        def rms_fm(ps, ones_t, inv_n):
            act(SQ.ap, ps.ap, AF.Square, [ps], [SQ])
            p2 = self.ps_alloc()
            mm(p2, p2.ap, ones_t.ap, SQ.ap, True, True, [ones_t, SQ])
            act(RS.ap, p2.ap, AF.Ln, [p2, EPST], [RS], scale=inv_n, bias=EPST[:, 0:1])
            self.ps_release(p2)
            act(RS.ap, RS.ap, AF.Exp, [RS], [RS], scale=-0.5)

        for l in range(L):
            src = x if l == 0 else xs
            dst = xs if l == 0 else y
            wl = w_in[l]
            S2d = S.ap.rearrange("p h v -> p (h v)")
            p.op("vector", lambda e: e.memset(S.ap, 0.0), writes=[S])
            p.op("vector", lambda e: e.memset(DL.ap, 0.0), writes=[DL])
            for blk in range(self.nblk):
                r0 = blk * TB
                xsb = Tl(None, [self.xs_t.bufs[blk]])
                for t in range(4):
                    dma("sync", self.X[:, t], src[r0 + t * 128:r0 + (t + 1) * 128, :],
                        [xsb] if l == 1 else [], [Tl(None, [self.X.bufs[t]])])
                norm(GMIX, l)
                for hp in range(2):
                    wt = self.wpanel()
                    load_cols(wt, 0, wl, O_I + hp * 512, 512)
                    for t in range(4):
                        ps = self.ps_alloc()
                        for kc in range(16):
                            mm(ps, ps.ap, self.HT[:, kc, t * 128:(t + 1) * 128], wt[:, kc, :], kc == 0, kc == 15, [wt, HTall])
                        cp(evac_eng(), VT[:, t, hp * 512:(hp + 1) * 512], ps.ap, [ps], [VTt[t]])
                        self.ps_release(ps)
                for hp in range(2):
                    wt = self.wpanel()
                    load_cols(wt, 0, wl, O_F + hp * 512, 512)
                    for hh in range(4):
                        h = hp * 4 + hh
                        gemm_fm(wt, hh * 128, lambda ps: act(KK.ap, ps.ap, AF.Sigmoid, [ps], [KK], scale=-1.0))
                        tsm("vector", KK.ap, KK.ap, OML[:, l, h:h + 1], [KK, OML], [KK])
                        act(LF.ap, KK.ap, AF.Ln, [KK, ONE1], [LF], scale=-1.0, bias=ONE1[:, 0:1])
                        p.op("vector", lambda e: e.tensor_tensor_scan(CUM.ap, SCANM.ap, LF.ap, 0.0, ALU.mult, ALU.add),
                             reads=[SCANM, LF], writes=[CUM])
                        C3 = CUM.ap.rearrange("p (c s) -> p c s", s=64)
                        tt("vector", EQ.ap.rearrange("p (c s) -> p c s", s=64), C3[:, :, 63:64].to_broadcast([128, 8, 64]), C3,
                           ALU.subtract, [CUM], [EQ])
                        act(EQ.ap, EQ.ap, AF.Exp, [EQ], [EQ])
                        tt("vector", KSb.ap, KK.ap, EQ.ap, ALU.mult, [KK, EQ], [KSb])
                        act(CH[:, 1], C3[:, :, 63], AF.Exp, [CUM], [CH])
                        for c in range(8):
                            tt("gpsimd", DL[:, h:h + 1], DL[:, h:h + 1], CUM[:, c * 64 + 63:c * 64 + 64], ALU.add, [DL, CUM], [DL])
                        Sh = Tl(None, [S.bufs[h]])
                        for pr in range(4):
                            pb = self.psb_alloc()
                            tr(pb, pb[:, 0:128], KSb[:, pr * 128:(pr + 1) * 128], IDb.ap, [KSb, IDb])
                            cp("scalar", KST[:, pr], pb[:, 0:128], [pb], [KST])
                            self.psb_release(pb)
                            for cc in range(2):
                                c = pr * 2 + cc
                                pd = self.ps_alloc()
                                mm(pd, pd[:, 0:128], KST[cc * 64:(cc + 1) * 64, pr], VT[cc * 64:(cc + 1) * 64, pr, h * 128:(h + 1) * 128],
                                   True, True, [KST, VTt[pr]])
                                stt("vector", S[:, h], S[:, h], CH[:, 1, c:c + 1], pd[:, 0:128], ALU.mult, ALU.add, [Sh, CH, pd], [Sh])
                                self.ps_release(pd)
                if blk == self.nblk - 1:
                    wt = self.wpanel()
                    for g in range(2):
                        for rr in range(2):
                            load_cols(wt, g * 128 + rr * 64, wl, O_AK + g * 64, 64)
                            load_cols(wt, 256 + g * 128 + rr * 64, wl, O_AV + g * 64, 64)
                    for g in range(2):
                        def kev0(ps, g=g):
                            rms_fm(ps, BDb, 1.0 / 64)
                            stt("vector", KT[g][:, 4], ps[:, 384:512], SMALL[:, l, 2:3], RS[:, 384:512],
                                ALU.mult, ALU.mult, [ps, SMALL, RS], [Tl(None, [KT[g].bufs[4]])])
                        gemm_fm(wt, g * 128, kev0)
                    ps = self.ps_alloc()
                    for kc in range(16):
                        mm(ps, ps[:, 0:256], self.HT[:, kc, 384:512], wt[:, kc, 256:512], kc == 0, kc == 15, [wt, HTall])
                    cp(evac_eng(), V2[:, 4], ps[:, 0:256], [ps], [Tl(None, [V2.bufs[4]])])
                    self.ps_release(ps)
            dma("sync", cin[:, 0:1024], S2d, [S], [cin_t])
            dma("sync", cin[:, 1024:1032], DL.ap, [DL], [cin_t])
            for g in range(2):
                dma("gpsimd", cin[:, 1032 + g * 128:1032 + (g + 1) * 128], KT[g][:, 4], [Tl(None, [KT[g].bufs[4]])], [cin_t])
            dma("gpsimd", cin[:, 1288:1544], V2[:, 4], [Tl(None, [V2.bufs[4]])], [cin_t])
            p.op("gpsimd", lambda e: e.collective_compute("AllGather", ALU.bypass, replica_groups=[list(range(8))],
                                                          ins=[cin.opt()], outs=[cout.opt()]),
                 reads=[cin_t], writes=[cout_t], dma=True, inc=1)
            p.op("vector", lambda e: e.memset(S.ap, 0.0), writes=[S])
            for r in range(8):
                dma("sync", UR.ap, cout[r * 128:(r + 1) * 128, 0:1024], [cout_t], [UR])
                dma("sync", DLR.ap, cout[r * 128:(r + 1) * 128, 1024:1032], [cout_t], [DLR])
                act(DM.ap, DLR.ap, AF.Exp, [DLR], [DM])
                ts("vector", DM.ap, DM.ap, -1.0, CM[:, r:r + 1], ALU.add, ALU.mult, [DM, CM], [DM])
                p.op("vector", lambda e: e.tensor_scalar_add(DM.ap, DM.ap, 1.0), reads=[DM], writes=[DM])
                tt("vector", S.ap, S.ap, DM.ap.rearrange("p (h o) -> p h o", o=1).to_broadcast([128, 8, 128]), ALU.mult, [S, DM], [S])
                stt("vector", S2d, UR.ap, CM[:, r:r + 1], S2d, ALU.mult, ALU.add, [UR, CM, S], [S])
            cp("scalar", Sb.ap, S.ap, [S], [Sb])
            c3 = cout.rearrange("(r p) n -> p r n", p=128)
            dma("sync", KTL.ap, c3[:, :, 1032:1288], [cout_t], [KTL])
            dma("sync", VTL.ap, c3[:, :, 1288:1544], [cout_t], [VTL])
            p.op("vector", lambda e: e.memset(T1[:, 0:256], 0.0), writes=[T1])
            p.op("vector", lambda e: e.memset(T2[:, 0:256], 0.0), writes=[T2])
            for r in range(8):
                stt("vector", T1[:, 0:256], KTL[:, r], CM[:, 8 + r:9 + r], T1[:, 0:256], ALU.mult, ALU.add, [KTL, CM, T1], [T1])
                stt("vector", T2[:, 0:256], VTL[:, r], CM[:, 8 + r:9 + r], T2[:, 0:256], ALU.mult, ALU.add, [VTL, CM, T2], [T2])
            for g in range(2):
                cp("vector", KT[g][:, 0], T1[:, g * 128:(g + 1) * 128], [T1], [Tl(None, [KT[g].bufs[0]])])
            cp("vector", V2[:, 0], T2[:, 0:256], [T2], [Tl(None, [V2.bufs[0]])])
            for blk in range(self.nblk):
                r0 = blk * TB
                xsb = Tl(None, [self.xs_t.bufs[blk]])
                for t in range(4):
                    dma("sync", self.X[:, t], src[r0 + t * 128:r0 + (t + 1) * 128, :],
                        [xsb] if l == 1 else [], [Tl(None, [self.X.bufs[t]])])
                try:
                    norm(GMIX, l)
                    if self.stop == 1:
                        raise _Stop()
                    for hp in range(2):
                        wt = self.wpanel()
                        load_cols(wt, 0, wl, O_I + hp * 512, 512)
                        for t in range(4):
                            ps = self.ps_alloc()
                            for kc in range(16):
                                mm(ps, ps.ap, self.HT[:, kc, t * 128:(t + 1) * 128], wt[:, kc, :], kc == 0, kc == 15, [wt, HTall])
                            cp(evac_eng(), VT[:, t, hp * 512:(hp + 1) * 512], ps.ap, [ps], [VTt[t]])
                            self.ps_release(ps)
                    if self.stop == 2:
                        raise _Stop()
                    for h in range(8):
                        wt = self.wpanel()
                        load_cols(wt, 0, wl, O_Q + h * 128, 128)
                        load_cols(wt, 128, wl, O_F + h * 128, 128)
                        load_cols(wt, 256, wl, O_G + h * 128, 128)
                        gemm_fm(wt, 0, lambda ps: act(QS.ap, ps.ap, AF.Silu, [ps], [QS]))
                        gemm_fm(wt, 128, lambda ps: act(KK.ap, ps.ap, AF.Sigmoid, [ps], [KK], scale=-1.0))
                        tsm("vector", KK.ap, KK.ap, OML[:, l, h:h + 1], [KK, OML], [KK])
                        act(LF.ap, KK.ap, AF.Ln, [KK, ONE1], [LF], scale=-1.0, bias=ONE1[:, 0:1])
                        p.op("vector", lambda e: e.tensor_tensor_scan(CUM.ap, SCANM.ap, LF.ap, 0.0, ALU.mult, ALU.add),
                             reads=[SCANM, LF], writes=[CUM])
                        C3 = CUM.ap.rearrange("p (c s) -> p c s", s=64)
                        tt("vector", LF.ap.rearrange("p (c s) -> p c s", s=64), C3, C3[:, :, 31:32].to_broadcast([128, 8, 64]),
                           ALU.subtract, [CUM], [LF])
                        act(EQ.ap, LF.ap, AF.Exp, [LF], [EQ])
                        act(EK.ap, LF.ap, AF.Exp, [LF], [EK], scale=-1.0)
                        act(CH[:, 0], C3[:, :, 31], AF.Exp, [CUM], [CH])
                        act(CH[:, 1], C3[:, :, 63], AF.Exp, [CUM], [CH])
                        tt("vector", CH[:, 2], C3[:, :, 63], C3[:, :, 31], ALU.subtract, [CUM], [CH])
                        act(CH[:, 2], CH[:, 2], AF.Exp, [CH], [CH])
                        tt("vector", QAb.ap, QS.ap, EQ.ap, ALU.mult, [QS, EQ], [QAb])
                        tt("gpsimd", QOb.ap.rearrange("p (c s) -> p c s", s=64), QAb.ap.rearrange("p (c s) -> p c s", s=64),
                           CH[:, 0].rearrange("p (c o) -> p c o", o=1).to_broadcast([128, 8, 64]), ALU.mult, [QAb, CH], [QOb])
                        tt("vector", KAb.ap, KK.ap, EK.ap, ALU.mult, [KK, EK], [KAb])
                        tt("gpsimd", KSb.ap.rearrange("p (c s) -> p c s", s=64), KAb.ap.rearrange("p (c s) -> p c s", s=64),
                           CH[:, 2].rearrange("p (c o) -> p c o", o=1).to_broadcast([128, 8, 64]), ALU.mult, [KAb, CH], [KSb])
                        Sh = Tl(None, [S.bufs[h]]); Sbh = Tl(None, [Sb.bufs[h]])
                        po = self.ps_alloc()
                        for pr in range(4):
                            tsl = slice(pr * 128, (pr + 1) * 128)
                            pb = self.psb_alloc()
                            tr(pb, pb[:, 0:128], KSb[:, tsl], IDb.ap, [KSb, IDb])
                            cp("scalar", KST[:, pr], pb[:, 0:128], [pb], [KST])
                            self.psb_release(pb)
                            pa = self.ps_alloc()
                            mm(pa, pa[:, 0:128], KAb[:, tsl], QAb[:, tsl], True, True, [KAb, QAb])
                            tt("vector", AM[:, pr], pa[:, 0:128], AMb.ap, ALU.mult, [pa, AMb], [AM])
                            self.ps_release(pa)
                            for cc in range(2):
                                c = pr * 2 + cc
                                csl = slice(c * 64, (c + 1) * 64)
                                mm(po, po[:, csl], VT[:, pr, h * 128:(h + 1) * 128], AM[:, pr, cc * 64:(cc + 1) * 64],
                                   True, False, [VTt[pr], AM])
                                mm(po, po[:, csl], Sb[:, h], QOb[:, csl], False, True, [Sbh, QOb])
                                pd = self.ps_alloc()
                                mm(pd, pd[:, 0:128], KST[cc * 64:(cc + 1) * 64, pr], VT[cc * 64:(cc + 1) * 64, pr, h * 128:(h + 1) * 128],
                                   True, True, [KST, VTt[pr]])
                                stt("vector", S[:, h], S[:, h], CH[:, 1, c:c + 1], pd[:, 0:128], ALU.mult, ALU.add, [Sh, CH, pd], [Sh])
                                self.ps_release(pd)
                                cp("scalar", Sb[:, h], S[:, h], [Sh], [Sbh])
                        rms_fm(po, ONESb, 1.0 / 128)
                        stt("vector", T1.ap, po.ap, SMALL[:, l, 0:1], RS.ap, ALU.mult, ALU.mult, [po, SMALL, RS], [T1])
                        self.ps_release(po)
                        gemm_fm(wt, 256, lambda ps: act(T2.ap, ps.ap, AF.Silu, [ps], [T2]))
                        tt("vector", OHG[:, h], T1.ap, T2.ap, ALU.mult, [T1, T2], [OHGh[h]])
                    if self.stop == 3:
                        raise _Stop()
                    wt = self.wpanel()
                    for g in range(2):
                        for rr in range(2):
                            load_cols(wt, g * 128 + rr * 64, wl, O_AK + g * 64, 64)
                            load_cols(wt, 256 + g * 128 + rr * 64, wl, O_AV + g * 64, 64)
                    for g in range(2):
                        def kev(ps, g=g):
                            rms_fm(ps, BDb, 1.0 / 64)
                            for t in range(4):
                                stt("vector", KT[g][:, 1 + t], ps[:, t * 128:(t + 1) * 128], SMALL[:, l, 2:3], RS[:, t * 128:(t + 1) * 128],
                                    ALU.mult, ALU.mult, [ps, SMALL, RS], [Tl(None, [KT[g].bufs[1 + t]])])
                        gemm_fm(wt, g * 128, kev)
                    for t in range(4):
                        ps = self.ps_alloc()
                        for kc in range(16):
                            mm(ps, ps[:, 0:256], self.HT[:, kc, t * 128:(t + 1) * 128], wt[:, kc, 256:512], kc == 0, kc == 15, [wt, HTall])
                        cp(evac_eng(), V2[:, 1 + t], ps[:, 0:256], [ps], [Tl(None, [V2.bufs[1 + t]])])
                        self.ps_release(ps)
                    for qp in range(2):
                        wt = self.wpanel()
                        load_cols(wt, 0, wl, O_AQ + qp * 512, 512)
                        for jj in range(4):
                            j = qp * 4 + jj

                            def qev(ps, j=j):
                                rms_fm(ps, BDb, 1.0 / 64)
                                stt("vector", QH[:, j], ps.ap, QN2S[:, l:l + 1], RS.ap, ALU.mult, ALU.mult, [ps, QN2S, RS], [QHj[j]])
                            gemm_fm(wt, jj * 128, qev)
                    if self.stop == 4:
                        raise _Stop()
                    pi = 0
                    for g in range(2):
                        QHg = Tl(None, [b for j in range(4 * g, 4 * g + 4) for b in QHj[j].bufs])
                        for rr in range(2):
                            rs_ = slice(rr * 64, (rr + 1) * 64)
                            bias_t = BIAS[g * 2 + rr]
                            B4 = bias_t.ap.rearrange("p (k n) -> p k n", k=2)
                            for t in range(4):
                                Pt = PT[pi % 2]
                                pi += 1
                                for kt in range(2):
                                    slot = t + kt
                                    ps = self.ps_alloc()
                                    mm(ps, ps.ap, KT[g][rs_, slot], QH[rs_, 4 * g:4 * g + 4, t * 128:(t + 1) * 128], True, True,
                                       [Tl(None, [KT[g].bufs[slot]]), QHg])
                                    tb = T1B if kt else T1
                                    if blk == 0 and t == 0 and kt == 0:
                                        stt("vector", tb.ap, ps.ap, HALO[:, 0:1], B4[:, kt], ALU.add, ALU.add, [ps, HALO, bias_t], [tb])
                                    else:
                                        tt("vector", tb.ap, ps.ap, B4[:, kt], ALU.add, [ps, bias_t], [tb])
                                    self.ps_release(ps)
                                    act(Pt[:, kt], tb.ap, AF.Exp, [tb], [Pt])
                                pn = self.ps_alloc()
                                pdn = self.ps_alloc()
                                for kt in range(2):
                                    slot = t + kt
                                    mm(pn, pn.ap, V2[:, slot, g * 128:(g + 1) * 128], Pt[:, kt], kt == 0, kt == 1,
                                       [Tl(None, [V2.bufs[slot]]), Pt])
                                for kt in range(2):
                                    mm(pdn, pdn.ap, ONESb.ap, Pt[:, kt], kt == 0, kt == 1, [ONESb, Pt])
                                d3 = DEN[rs_, :].rearrange("p (j q) -> p j q", j=4)
                                tt("vector", d3, pdn[rs_, :].rearrange("p (j q) -> p j q", j=4),
                                   ESINK[rs_, l, 4 * g:4 * g + 4].rearrange("p (j o) -> p j o", o=1).to_broadcast([64, 4, 128]),
                                   ALU.add, [pdn, ESINK], [DEN])
                                self.ps_release(pdn)
                                p.op("vector", lambda e, rs_=rs_: e.reciprocal(DEN[rs_, :], DEN[rs_, :]), reads=[DEN], writes=[DEN])
                                tt("vector", ATT[rs_, 4 * g:4 * g + 4, t * 128:(t + 1) * 128], pn[rs_, :].rearrange("p (j q) -> p j q", j=4),
                                   d3, ALU.mult, [pn, DEN], [Tl(None, [b for j in range(4 * g, 4 * g + 4) for b in ATTj[j].bufs])])
                                self.ps_release(pn)
                    if self.stop == 5:
                        raise _Stop()
                    for g in range(2):
                        cp("gpsimd", KT[g][:, 0], KT[g][:, 4], [Tl(None, [KT[g].bufs[4]])], [Tl(None, [KT[g].bufs[0]])])
                    cp("gpsimd", V2[:, 0], V2[:, 4], [Tl(None, [V2.bufs[4]])], [Tl(None, [V2.bufs[0]])])
                    OHGall = Tl(None, [b for t_ in OHGh for b in t_.bufs])
                    ATTall = Tl(None, [b for t_ in ATTj for b in t_.bufs])
                    for jj in range(4):
                        wt = self.wpanel()
                        load_cols(wt, 0, wl, O_GH + jj * 512, 512)
                        for c in range(4):
                            gemm_fm(wt, c * 128, lambda ps, c=c: act(SG1[:, c], ps.ap, AF.Sigmoid, [ps], [SG1c[c]]))
                        wt = self.wpanel()
                        load_cols(wt, 0, wl, O_GA + jj * 512, 512)
                        for c in range(4):
                            gemm_fm(wt, c * 128, lambda ps, c=c: act(SGA[:, c], ps.ap, AF.Sigmoid, [ps], [SGAc[c]]))
                        wt = self.wpanel()
                        dma("sync", wt[:, 0:8, :], w_hg[l].rearrange("(kc p) n -> p kc n", p=128)[:, :, jj * 512:(jj + 1) * 512], [WHG_T], [wt])
                        dma("sync", wt[:, 8:16, :], w_at[l].rearrange("(kc p) n -> p kc n", p=128)[:, :, jj * 512:(jj + 1) * 512], [WAT_T], [wt])
                        for c in range(4):
                            j16 = jj * 4 + c
                            ps = self.ps_alloc()
                            for kc in range(8):
                                mm(ps, ps.ap, wt[:, kc, c * 128:(c + 1) * 128], OHG[:, kc], kc == 0, kc == 7, [wt, OHGall])
                            tt("vector", T1.ap, ps.ap, SG1[:, c], ALU.mult, [ps, SG1c[c]], [T1])
                            self.ps_release(ps)
                            ps = self.ps_alloc()
                            for kc in range(8):
                                mm(ps, ps.ap, wt[:, 8 + kc, c * 128:(c + 1) * 128], ATT[:, kc], kc == 0, kc == 7, [wt, ATTall])
                            tt("vector", T2.ap, ps.ap, SGA[:, c], ALU.mult, [ps, SGAc[c]], [T2])
                            self.ps_release(ps)
                            tt("gpsimd", MIX[:, j16], T1.ap, T2.ap, ALU.add, [T1, T2], [MIXc[j16]])
                    if self.stop == 6:
                        raise _Stop()
                    MIXall = Tl(None, [b for t_ in MIXc for b in t_.bufs])
                    for cg in range(4):
                        wt = self.wpanel()
                        load_cols(wt, 0, w_out[l], cg * 512, 512, WOUT_T)
                        for t in range(4):
                            ps = self.ps_alloc()
                            for kc in range(16):
                                mm(ps, ps.ap, MIX[:, kc, t * 128:(t + 1) * 128], wt[:, kc, :], kc == 0, kc == 15, [wt, MIXall])
                            xt = Tl(None, [self.X.bufs[t]])
                            tt("vector", self.X[:, t, cg * 512:(cg + 1) * 512], self.X[:, t, cg * 512:(cg + 1) * 512], ps.ap, ALU.add, [ps, xt], [xt])
                            self.ps_release(ps)
                    if self.stop == 7:
                        raise _Stop()
                    norm(GFFN, l)
                    for fg in range(16):
                        wt = self.wpanel()
                        load_cols(wt, 0, w_up[l], fg * 512, 512, WUP_T)
                        for c in range(4):
                            fc = fg * 4 + c

                            def uev(ps, fc=fc):
                                tb = T1B if fc % 2 else T1
                                act(tb.ap, ps.ap, AF.Relu, [ps], [tb])
                                tt("gpsimd" if fc % 2 else "vector", AF_[:, fc], tb.ap, tb.ap, ALU.mult, [tb], [AFc[fc]])
                            gemm_fm(wt, c * 128, uev)
                    if self.stop == 8:
                        raise _Stop()
                    for cg in range(4):
                        pss = [self.ps_alloc() for _ in range(4)]
                        for fg in range(4):
                            wt = self.wpanel()
                            srcw = w_dn[l][fg * 2048:(fg + 1) * 2048, :].rearrange("(f p) n -> p f n", p=128)[:, :, cg * 512:(cg + 1) * 512]
                            dma("sync", wt.ap, srcw, [WDN_T], [wt])
                            for c in range(4):
                                for f in range(16):
                                    fc = fg * 16 + f
                                    mm(pss[c], pss[c].ap, wt[:, f, c * 128:(c + 1) * 128], AF_[:, fc], fg == 0 and f == 0, fg == 3 and f == 15,
                                       [wt, AFc[fc]])
                        for c in range(4):
                            yt = YT[(cg % 2) * 4 + c]
                            cp(evac_eng(), yt.ap, pss[c].ap, [pss[c]], [yt])
                            self.ps_release(pss[c])
                        for t in range(4):
                            pt = self.ps_alloc()
                            for c in range(4):
                                yt = YT[(cg % 2) * 4 + c]
                                tr(pt, pt[:, c * 128:(c + 1) * 128], yt[:, t * 128:(t + 1) * 128], IDf.ap, [yt, IDf])
                            xt = Tl(None, [self.X.bufs[t]])
                            tt("vector", self.X[:, t, cg * 512:(cg + 1) * 512], self.X[:, t, cg * 512:(cg + 1) * 512], pt.ap, ALU.add, [pt, xt], [xt])
                            self.ps_release(pt)
                except _Stop:
                    pass
                for t in range(4):
                    dma("sync", dst[r0 + t * 128:r0 + (t + 1) * 128, :], self.X[:, t], [Tl(None, [self.X.bufs[t]])],
                        [xsb] if l == 0 else [], final=(l == L - 1))
        self.p.emit()
        self.st.close()
        return self.nc


_CACHE = {}


def _layout_small(norm_mix, norm_ffn, lb_logits, hg_norm, q_norm, k_norm, sinks):
    f = np.float32
    gmix = np.ascontiguousarray(norm_mix.reshape(L, 16, 128).transpose(0, 2, 1)).astype(f)
    gffn = np.ascontiguousarray(norm_ffn.reshape(L, 16, 128).transpose(0, 2, 1)).astype(f)
    lbl = np.ascontiguousarray(lb_logits.reshape(L, 8, 128).transpose(0, 2, 1)).astype(f)
    small = np.zeros((L, 128, 16), f)
    small[:, :, 0] = hg_norm
    small[:, :, 1] = np.concatenate([q_norm, q_norm], axis=1)
    small[:, :, 2] = np.concatenate([k_norm, k_norm], axis=1)
    for j in range(8):
        small[:, 0:64, 8 + j] = sinks[:, 2 * j][:, None]
        small[:, 64:128, 8 + j] = sinks[:, 2 * j + 1][:, None]
    return gmix, gffn, lbl, small


NSEG = 4


def run(x, norm_mix, w_in, lb_logits, hg_norm, q_norm, k_norm, sinks, w_hg_out, w_at_out, w_out,
        norm_ffn, w_up, w_down, trace=False):
    x = np.asarray(x, np.float32)
    nseq, T, _ = x.shape
    assert nseq * NSEG == 8
    Tc = T // NSEG
    nblk = Tc // TB
    if nblk not in _CACHE:
        _CACHE[nblk] = Builder(nblk).build()
    nc = _CACHE[nblk]
    mats, scanmask, bias = host_consts()
    gmix, gffn, lbl, small = _layout_small(*[np.asarray(a, np.float32) for a in
                                             (norm_mix, norm_ffn, lb_logits, hg_norm, q_norm, k_norm, sinks)])
    base = dict(gmix=gmix, gffn=gffn, lbl=lbl, small=small, cmats=mats, cscan=scanmask, cbias=bias)
    wts = dict(w_in=w_in, w_hg_out=w_hg_out, w_at_out=w_at_out, w_out=w_out, w_up=w_up, w_down=w_down)
    wsh = {}
    for k, v in wts.items():
        v = np.asarray(v, np.float32)
        v2 = v.reshape(v.shape[0] * v.shape[1], v.shape[2])
        wsh[k] = np.split(v2, 8, axis=0)
    in_maps = []
    for c in range(8):
        b, sg = c // NSEG, c % NSEG
        halo = np.zeros((128, 8), np.float32)
        halo[:, 0] = NEG if sg == 0 else 0.0
        cm = np.zeros((128, 16), np.float32)
        for r in range(8):
            if r // NSEG == b and r % NSEG < sg:
                cm[:, r] = 1.0
            if r // NSEG == b and r == c - 1:
                cm[:, 8 + r] = 1.0
        m = dict(base, x=np.ascontiguousarray(x[b, sg * Tc:(sg + 1) * Tc]), halo=halo, cmask=cm)
        for k in wsh:
            m[k] = np.ascontiguousarray(wsh[k][c])
        in_maps.append(m)
    res = run_bass_kernel_spmd(nc, in_maps, core_ids=list(range(8)), **({"trace": True} if trace else {}))
    out = np.stack([np.concatenate([np.asarray(res.results[b * NSEG + sg]["y"]) for sg in range(NSEG)], 0)
                    for b in range(nseq)], 0)
    return out, res


def kernel(x, norm_mix, w_in, lb_logits, hg_norm, q_norm, k_norm, sinks, w_hg_out, w_at_out, w_out,
           norm_ffn, w_up, w_down):
    out, _ = run(x, norm_mix, w_in, lb_logits, hg_norm, q_norm, k_norm, sinks, w_hg_out, w_at_out, w_out,
                 norm_ffn, w_up, w_down)
    return out.astype(np.float32)
```
